# Optimizing a Trainium2 kernel written in Bass

```python
import math
import jax, jax.numpy as jnp
from jax import lax
import numpy as np

D_MODEL = 1024
BATCH = 8
SEQ = 4096
DEPTH = 1

CTX_LEN = 256
GRID_W = 64
D_MIX = D_MODEL
D_S5 = D_MIX // 2
D_HY = D_MIX - D_S5
S5_GROUP = 16
S5_GROUPS = D_S5 // S5_GROUP
S5_STATE = 64
S5_DT_MIN = 1e-3
S5_DT_MAX = 1e-1
HY_ORDER = 2
HY_BANDS = 16
HY_EMB = 1 + 2 * HY_BANDS
HY_FILTER_HIDDEN = 64
HY_DECAY_TARGET = 1e-2
HY_FAST_DECAY = 0.3
HY_SLOW_DECAY = 1.5
SHORT_CONV = 3
D_FF = 4 * D_MODEL
N_MOD = 6
POS_BASE = 10000.0
EPS = 1e-6

kernel_name = "hymba_s5_hyena_prefix_dit_block"


def rmsnorm(x, g):
    xf = x.astype(jnp.float32)
    y = xf * lax.rsqrt(jnp.mean(xf * xf, axis=-1, keepdims=True) + EPS)
    return (y * g.astype(jnp.float32)).astype(x.dtype)


def pos_embed_2d(n, d):
    rows = n // GRID_W
    row = jnp.repeat(jnp.arange(rows, dtype=jnp.float32), GRID_W)
    col = jnp.tile(jnp.arange(GRID_W, dtype=jnp.float32), rows)
    quarter = d // 4
    omega = 1.0 / (POS_BASE ** (jnp.arange(quarter, dtype=jnp.float32) / quarter))

    def enc(p):
        ang = p[:, None] * omega[None, :]
        return jnp.concatenate([jnp.sin(ang), jnp.cos(ang)], axis=-1)

    return jnp.concatenate([enc(row), enc(col)], axis=-1)


def _linear_recurrence(left, right):
    a1, b1 = left
    a2, b2 = right
    return a1 * a2, a2 * b1 + b2


def s5_discretize(a_re, a_im, log_step, b_re, b_im):
    lam = lax.complex(a_re.astype(jnp.float32), a_im.astype(jnp.float32))
    step = jnp.exp(log_step.astype(jnp.float32))[:, None]
    lam_bar = jnp.exp(lam * step)
    bmat = lax.complex(b_re.astype(jnp.float32), b_im.astype(jnp.float32))
    b_bar = ((lam_bar - 1.0) / lam)[..., None] * bmat
    return lam_bar, b_bar


def s5_scan(u, lam_bar, b_bar, s0, reverse):
    bu = jnp.einsum('blgc,gpc->lbgp', u.astype(jnp.complex64), b_bar)
    if s0 is not None:
        edge = bu.shape[0] - 1 if reverse else 0
        bu = bu.at[edge].add(lam_bar[None] * s0)
    a = jnp.broadcast_to(lam_bar[None, None], (bu.shape[0], 1) + lam_bar.shape)
    _, states = lax.associative_scan(_linear_recurrence, (a, bu), reverse=reverse)
    return states


def s5_mixer(u, u_ctx, a_re, a_im, log_step, b_re, b_im, c_re, c_im, d_skip, glu_w, glu_b):
    bsz, n, _ = u.shape
    uf = u.astype(jnp.float32).reshape(bsz, n, S5_GROUPS, S5_GROUP)
    uc = u_ctx.astype(jnp.float32).reshape(bsz, u_ctx.shape[1], S5_GROUPS, S5_GROUP)
    outs = []
    for direction, reverse in enumerate((False, True)):
        lam_bar, b_bar = s5_discretize(a_re[direction], a_im[direction], log_step[direction],
                                       b_re[direction], b_im[direction])
        ctx_states = s5_scan(uc, lam_bar, b_bar, None, reverse)
        s0 = ctx_states[0] if reverse else ctx_states[-1]
        states = s5_scan(uf, lam_bar, b_bar, s0, reverse)
        c_mat = lax.complex(c_re[direction].astype(jnp.float32),
                            c_im[direction].astype(jnp.float32))
        outs.append(jnp.real(jnp.einsum('lbgp,gcp->blgc', states, c_mat)))
    y = outs[0] + outs[1] + uf * d_skip.astype(jnp.float32).reshape(S5_GROUPS, S5_GROUP)
    y = jax.nn.gelu(y.reshape(bsz, n, D_S5))
    ab = jnp.einsum('bld,de->ble', y, glu_w.astype(jnp.float32)) + glu_b.astype(jnp.float32)
    val, gate = jnp.split(ab, 2, axis=-1)
    return (val * jax.nn.sigmoid(gate)).astype(u.dtype)


def short_conv(x, w, b):
    xp = jnp.pad(x, ((0, 0), (1, 1), (0, 0)))
    return xp[:, :-2] * w[0] + xp[:, 1:-1] * w[1] + xp[:, 2:] * w[2] + b


def hyena_filters(n, w1, b1, w2, b2, freq, w3, decay):
    f32 = jnp.float32
    t = jnp.linspace(0.0, 1.0, n, dtype=f32)[:, None]
    w = 2.0 * math.pi * jnp.arange(n, dtype=f32) / n
    bands = jnp.linspace(1e-4, HY_BANDS - 1, HY_BANDS, dtype=f32)
    ang = w[:, None] * bands[None, :]
    emb = jnp.concatenate([t, jnp.cos(ang), -jnp.sin(ang)], axis=-1)
    fr = freq.astype(f32)
    h = jnp.sin(fr * (emb @ w1.astype(f32) + b1.astype(f32)))
    h = jnp.sin(fr * (h @ w2.astype(f32) + b2.astype(f32)))
    h = (h @ w3.astype(f32)).reshape(n, HY_ORDER, 2, D_HY)
    window = jnp.exp(-t[:, :, None] * jnp.abs(decay.astype(f32))[None])
    return h * window[:, :, None, :]


def fft_conv_bidir(u, h_fwd, h_bwd, bias):
    n = u.shape[1]
    filt = jnp.concatenate([h_fwd, jnp.zeros((1, h_fwd.shape[1]), h_fwd.dtype), h_bwd[:0:-1]], axis=0)
    u_f = jnp.fft.rfft(u, n=2 * n, axis=1)
    k_f = jnp.fft.rfft(filt, n=2 * n, axis=0)
    y = jnp.fft.irfft(u_f * k_f[None], n=2 * n, axis=1)[:, :n]
    return y + u * bias


def hyena_mixer(z_in, conv_w, conv_b, f_w1, f_b1, f_w2, f_b2, f_freq, f_w3, decay, bias):
    f32 = jnp.float32
    n = z_in.shape[1]
    zc = short_conv(z_in.astype(f32), conv_w.astype(f32), conv_b.astype(f32))
    v, x1, x2 = jnp.split(zc, 3, axis=-1)
    filt = hyena_filters(n, f_w1, f_b1, f_w2, f_b2, f_freq, f_w3, decay)
    z = v
    for order, gate in enumerate((x1, x2)):
        z = gate * fft_conv_bidir(z, filt[:, order, 0], filt[:, order, 1], bias[order].astype(f32))
    return z.astype(z_in.dtype)


def setup_inputs(seed: int = 0) -> dict:
    key = jax.random.key(seed)
    ks = iter(jax.random.split(key, 48))
    f32 = jnp.float32

    def nrm(shape, scale):
        return jax.random.normal(next(ks), shape, f32) * scale

    G, P, C, H = S5_GROUPS, S5_STATE, S5_GROUP, HY_FILTER_HIDDEN
    min_decay = math.log(HY_DECAY_TARGET) / HY_SLOW_DECAY
    max_decay = math.log(HY_DECAY_TARGET) / HY_FAST_DECAY
    decay_lin = jnp.abs(jnp.linspace(min_decay, max_decay, D_HY, dtype=f32))
    return {
        "x": nrm((BATCH, SEQ, D_MODEL), 1.0),
        "c": nrm((BATCH, D_MODEL), 1.0),
        "ctx": nrm((BATCH, CTX_LEN, D_MODEL), 1.0),
        "c_ctx": nrm((D_MODEL,), 1.0),
        "ada_w": nrm((DEPTH, D_MODEL, N_MOD * D_MODEL), 0.5 * D_MODEL ** -0.5),
        "ada_b": nrm((DEPTH, N_MOD * D_MODEL), 0.02),
        "norm1_g": 1.0 + nrm((DEPTH, D_MODEL), 0.02),
        "w_in": nrm((DEPTH, D_MODEL, D_S5 + 3 * D_HY), D_MODEL ** -0.5),
        "s5_a_re": -0.5 + nrm((DEPTH, 2, G, P), 0.01),
        "s5_a_im": math.pi * jnp.arange(P, dtype=f32) + nrm((DEPTH, 2, G, P), 0.01),
        "s5_log_step": jax.random.uniform(next(ks), (DEPTH, 2, G), f32,
                                          math.log(S5_DT_MIN), math.log(S5_DT_MAX)),
        "s5_b_re": nrm((DEPTH, 2, G, P, C), (2 * C) ** -0.5),
        "s5_b_im": nrm((DEPTH, 2, G, P, C), (2 * C) ** -0.5),
        "s5_c_re": nrm((DEPTH, 2, G, C, P), (2 * P) ** -0.5),
        "s5_c_im": nrm((DEPTH, 2, G, C, P), (2 * P) ** -0.5),
        "s5_d": nrm((DEPTH, D_S5), 1.0),
        "s5_glu_w": nrm((DEPTH, D_S5, 2 * D_S5), D_S5 ** -0.5),
        "s5_glu_b": nrm((DEPTH, 2 * D_S5), 0.02),
        "hy_conv_w": nrm((DEPTH, SHORT_CONV, 3 * D_HY), SHORT_CONV ** -0.5),
        "hy_conv_b": nrm((DEPTH, 3 * D_HY), 0.02),
        "hy_f_w1": nrm((DEPTH, HY_EMB, H), HY_EMB ** -0.5),
        "hy_f_b1": nrm((DEPTH, H), 0.02),
        "hy_f_w2": nrm((DEPTH, H, H), H ** -0.5),
        "hy_f_b2": nrm((DEPTH, H), 0.02),
        "hy_f_freq": 1.0 + nrm((DEPTH, H), 0.02),
        "hy_f_w3": nrm((DEPTH, H, HY_ORDER * 2 * D_HY), H ** -0.5),
        "hy_decay": decay_lin * (1.0 + nrm((DEPTH, HY_ORDER, D_HY), 0.02)),
        "hy_bias": nrm((DEPTH, HY_ORDER, D_HY), 1.0),
        "mix_g_s5": 1.0 + nrm((DEPTH, D_S5), 0.02),
        "mix_g_hy": 1.0 + nrm((DEPTH, D_HY), 0.02),
        "w_out": nrm((DEPTH, D_MIX, D_MODEL), D_MIX ** -0.5),
        "norm2_g": 1.0 + nrm((DEPTH, D_MODEL), 0.02),
        "mlp_w1": nrm((DEPTH, D_MODEL, D_FF), D_MODEL ** -0.5),
        "mlp_w2": nrm((DEPTH, D_FF, D_MODEL), D_FF ** -0.5),
        "final_g": 1.0 + nrm((D_MODEL,), 0.02),
    }


def reference(x, c, ctx, c_ctx, ada_w, ada_b, norm1_g, w_in,
              s5_a_re, s5_a_im, s5_log_step, s5_b_re, s5_b_im, s5_c_re, s5_c_im,
              s5_d, s5_glu_w, s5_glu_b,
              hy_conv_w, hy_conv_b, hy_f_w1, hy_f_b1, hy_f_w2, hy_f_b2, hy_f_freq,
              hy_f_w3, hy_decay, hy_bias,
              mix_g_s5, mix_g_hy, w_out, norm2_g, mlp_w1, mlp_w2, final_g):
    n = x.shape[1]
    h = x + pos_embed_2d(n, D_MODEL).astype(x.dtype)[None]
    c_act = jax.nn.silu(c)
    c_ctx_act = jax.nn.silu(c_ctx)
    for i in range(DEPTH):
        mod = c_act @ ada_w[i] + ada_b[i]
        shift1, scale1, gate1, shift2, scale2, gate2 = jnp.split(mod[:, None, :], N_MOD, axis=-1)
        mod_ctx = c_ctx_act @ ada_w[i] + ada_b[i]
        shift1_c, scale1_c = mod_ctx[:D_MODEL], mod_ctx[D_MODEL:2 * D_MODEL]

        hn = rmsnorm(h, norm1_g[i]) * (1.0 + scale1) + shift1
        proj = hn @ w_in[i]
        u_s5 = proj[..., :D_S5]
        z_hy = proj[..., D_S5:]
        cn = rmsnorm(ctx, norm1_g[i]) * (1.0 + scale1_c) + shift1_c
        u_ctx = cn @ w_in[i][:, :D_S5]

        y_s5 = s5_mixer(u_s5, u_ctx, s5_a_re[i], s5_a_im[i], s5_log_step[i],
                        s5_b_re[i], s5_b_im[i], s5_c_re[i], s5_c_im[i],
                        s5_d[i], s5_glu_w[i], s5_glu_b[i])
        y_hy = hyena_mixer(z_hy, hy_conv_w[i], hy_conv_b[i], hy_f_w1[i], hy_f_b1[i],
                           hy_f_w2[i], hy_f_b2[i], hy_f_freq[i], hy_f_w3[i],
                           hy_decay[i], hy_bias[i])
        mix = jnp.concatenate([rmsnorm(y_s5, mix_g_s5[i]), rmsnorm(y_hy, mix_g_hy[i])], axis=-1)
        h = h + gate1 * (mix @ w_out[i])

        hn2 = rmsnorm(h, norm2_g[i]) * (1.0 + scale2) + shift2
        hid = jnp.square(jax.nn.relu(hn2 @ mlp_w1[i]))
        h = h + gate2 * (hid @ mlp_w2[i])
    return rmsnorm(h, final_g)
```

```python
import math
import os
import numpy as np
import concourse.bass as bass
import concourse.mybir as mybir
from concourse.bass_utils import run_bass_kernel_spmd
from contextlib import ExitStack

F32 = mybir.dt.float32
BF16 = mybir.dt.bfloat16
AF = mybir.ActivationFunctionType
ALU = mybir.AluOpType

ENGS = ("pe", "act", "dve", "pool", "sp")
D = 1024
L = 4096
LC = 256
LT = L + LC
EPS = 1e-6


class Prog:
    def __init__(self, nc):
        self.nc = nc
        self.base = ExitStack()
        self.stacks = [self.base]
        self.ops = {e: [] for e in ENGS}
        self.cnt = {e: 0 for e in ENGS}
        self.sems = {}
        self.dcnt = {}
        self.last_w = {}
        self.readers = {}
        self.waited = {e: {} for e in ENGS}
        for e in ENGS:
            self.sems["E_" + e] = self.base.enter_context(nc.semaphore("E_" + e))

    def sb(self, name, shape, dt=F32):
        return self.stacks[-1].enter_context(self.nc.sbuf_tensor(name, list(shape), dt))

    def ps(self, name, shape, dt=F32):
        return self.stacks[-1].enter_context(self.nc.psum_tensor(name, list(shape), dt))

    def push(self):
        self.stacks.append(ExitStack())

    def pop(self):
        self.barrier()
        self.stacks.pop().close()

    def _sem(self, name):
        if name not in self.sems:
            self.sems[name] = self.base.enter_context(self.nc.semaphore(name))
            self.dcnt[name] = 0
        return self.sems[name]

    def _deps(self, eng, reads, writes):
        deps = {}

        def need(sv):
            if sv is None:
                return
            s, v = sv
            if deps.get(s, 0) < v:
                deps[s] = v
        for t in reads:
            need(self.last_w.get(t))
        for t in writes:
            need(self.last_w.get(t))
            for r in self.readers.get(t, ()):
                need(r)
        out = []
        for s, v in deps.items():
            if eng == "pe" and s == "E_pe":
                continue
            if eng in ("dve", "act") and s == "E_" + eng and v <= self.cnt[eng] - 1:
                continue
            if self.waited[eng].get(s, 0) >= v:
                continue
            self.waited[eng][s] = v
            out.append((s, v))
        return out

    def _mark(self, reads, writes, sv):
        for t in reads:
            self.readers.setdefault(t, []).append(sv)
        for t in writes:
            self.last_w[t] = sv
            self.readers[t] = []

    def op(self, eng, fn, reads=(), writes=()):
        waits = self._deps(eng, reads, writes)
        self.cnt[eng] += 1
        sv = ("E_" + eng, self.cnt[eng])
        self.ops[eng].append((fn, waits, ("E_" + eng, 1)))
        self._mark(reads, writes, sv)

    def dma(self, q, fn, dsem, reads=(), writes=()):
        self._sem(dsem)
        waits = self._deps(q, reads, writes)
        self.dcnt[dsem] += 16
        sv = (dsem, self.dcnt[dsem])
        self.ops[q].append((fn, waits, (dsem, 16)))
        self._mark(reads, writes, sv)

    def barrier(self):
        for e in ENGS:
            waits = []
            for f in ENGS:
                s = "E_" + f
                if f != e and self.cnt[f] > self.waited[e].get(s, 0):
                    waits.append((s, self.cnt[f]))
                    self.waited[e][s] = self.cnt[f]
            for s, v in self.dcnt.items():
                if v > self.waited[e].get(s, 0):
                    waits.append((s, v))
                    self.waited[e][s] = v
            self.ops[e].append((None, waits, None))

    def emit(self):
        nc = self.nc
        sems = self.sems
        ops = self.ops
        with nc.Block() as block:
            def runner(name):
                def run(e):
                    for fn, waits, inc in ops[name]:
                        for s, v in waits:
                            e.wait_ge(sems[s], v)
                        if fn is not None:
                            fn(e).then_inc(sems[inc[0]], inc[1])
                return run
            block.tensor(runner("pe"))
            block.scalar(runner("act"))
            block.vector(runner("dve"))
            block.gpsimd(runner("pool"))
            block.sync(runner("sp"))
        while self.stacks:
            self.stacks.pop().close()


def build_program(debug=False):
    STOP = int(os.environ.get('HY_STOP', '99'))
    nc = bass.Bass("TRN2", target_bir_lowering=False)
    P = Prog(nc)

    def din(name, shape, dt=F32):
        return nc.dram_tensor(name, list(shape), dt, kind="ExternalInput").ap()

    def dscr(name, shape, dt):
        return nc.dram_tensor(name, list(shape), dt, kind=("ExternalOutput" if debug else "Internal")).ap()

    x = din("x", [L, D]); ctx = din("ctx", [LC, D]); pos = din("pos", [L, D])
    cc = din("cc", [128, 8, 2])
    ada_w = din("ada_w", [D, 6 * D]); ada_b = din("ada_b", [1, 6 * D])
    g1 = din("g1", [1, D]); g2 = din("g2", [1, D]); fg = din("fg", [1, D])
    w_in = din("w_in", [D, 2048])
    s5p = din("s5p", [2, 3, 128, 16])
    bpad = din("bpad", [2, 128, 32 * 2 * 128])
    clay = din("clay", [2, 2, 128, 256])
    s5d = din("s5d", [128, 4])
    glu_w = din("glu_w", [512, 1024]); glu_b = din("glu_b", [128, 8])
    cw = din("cw", [128, 12, 3]); cb = din("cb", [128, 12])
    embT = din("embT", [2, 33, L]); tauX = din("tauX", [64, 2, 64])
    hdec_row = din("hdec_row", [1, 1024]); hbias_row = din("hbias_row", [1, 1024])
    Gtab = din("Gtab", [128, 128 * 2 * 128], BF16); GPtab = din("GPtab", [128, 128 * 128], BF16)
    F1tab = din("F1tab", [64, 512], BF16); Etab = din("Etab", [128, 128], BF16)
    fw1 = din("fw1", [33, 64]); fw2 = din("fw2", [64, 64]); fw3 = din("fw3", [64, 2048])
    fvec = din("fvec", [64, 3])
    mixg = din("mixg", [128, 8])
    w_out = din("w_out", [D, D]); w1 = din("w1", [D, 4 * D]); w2 = din("w2", [4 * D, D])
    out = nc.dram_tensor("out", [L, D], F32, kind="ExternalOutput").ap()

    hnT_d = dscr("hnT_d", [D, LT], BF16)
    z_d = dscr("z_d", [1536, L], F32)
    ymix_d = dscr("ymix_d", [D, L], BF16)
    w1b_d = dscr("w1b_d", [D, 4 * D], BF16)
    w2b_d = dscr("w2b_d", [4 * D, D], BF16)

    ident = P.sb("ident", [128, 128], BF16)
    ones = P.sb("ones", [128, 1], BF16)
    G1 = P.sb("G1", [128, D]); A2 = P.sb("A2", [128, D]); B2 = P.sb("B2", [128, D])
    G2 = P.sb("G2", [128, D]); FG = P.sb("FG", [128, D])
    rs = P.sb("rs", [128, 32]); rh = P.sb("rh", [128, 32])
    ssh = P.sb("ssh", [128, 32])
    P.op("pool", lambda e: e.memset(ident[:], 1.0), writes=["ident"])
    P.op("pool", lambda e: e.affine_select(out=ident[:], in_=ident[:], pattern=[[-1, 128]],
                                          compare_op=ALU.is_equal, fill=0.0, base=0, channel_multiplier=1),
         reads=["ident"], writes=["ident"])
    P.op("pool", lambda e: e.memset(ones[:], 1.0), writes=["ones"])
    P.op("pool", lambda e: e.memset(ssh[:], 0.0), writes=["ssh"])

    for i in range(4):
        P.dma("pool", lambda e, i=i: e.dma_start(out=w1b_d[i * 256:(i + 1) * 256, :], in_=w1[i * 256:(i + 1) * 256, :]),
              "D_w1c", writes=["w1b_d"] if i == 3 else ["w1b_d%d" % i])
        P.dma("pool", lambda e, i=i: e.dma_start(out=w2b_d[i * 1024:(i + 1) * 1024, :], in_=w2[i * 1024:(i + 1) * 1024, :]),
              "D_w2c", writes=["w2b_d"] if i == 3 else ["w2b_d%d" % i])

    P.push()
    A1 = P.sb("A1", [128, D]); B1 = P.sb("B1", [128, D]); A1c = P.sb("A1c", [128, D]); B1c = P.sb("B1c", [128, D])
    P.push()
    ccs = P.sb("ccs", [128, 8, 2]); sc = P.sb("sc", [128, 8, 2])
    screp = [P.sb("screp%d" % w, [128, 8, 128]) for w in range(2)]
    adab = P.sb("adab", [128, 6 * D])
    modc = P.sb("modc", [128, 6 * D]); modx = P.sb("modx", [128, 2 * D])
    grep = P.sb("grep", [128, D])
    wblk = [P.sb("wblk%d" % i, [128, 8, 512]) for i in range(2)]
    psm = [P.ps("psm%d" % i, [128, 512]) for i in range(2)]
    P.dma("sp", lambda e: e.dma_start(out=ccs[:], in_=cc), "D_ccs", writes=["ccs"])
    P.dma("sp", lambda e: e.dma_start(out=adab[:], in_=ada_b.to_broadcast([128, 6 * D])), "D_adab", writes=["adab"])
    P.op("act", lambda e: e.activation(out=sc[:], in_=ccs[:], func=AF.Silu), reads=["ccs"], writes=["sc"])
    for w in range(2):
        for kc in range(8):
            P.op("dve", lambda e, w=w, kc=kc: e.tensor_copy(out=screp[w][:, kc, :], in_=sc[:, kc, w:w + 1].to_broadcast([128, 128])),
                 reads=["sc"], writes=["screp%d" % w])
    adaw_v = ada_w.rearrange("(kc p) n -> p kc n", p=128)
    k = 0
    for blk in range(12):
        wb = wblk[blk % 2]
        P.dma("sp", lambda e, wb=wb, blk=blk: e.dma_start(out=wb[:], in_=adaw_v[:, :, blk * 512:(blk + 1) * 512]),
              "D_" + wb.name, writes=[wb.name])
        for w in range(2 if blk < 4 else 1):
            pm = psm[k % 2]; k += 1
            for kc in range(8):
                P.op("pe", lambda e, pm=pm, w=w, kc=kc, wb=wb: e.matmul(pm[:], lhsT=screp[w][:, kc, :], rhs=wb[:, kc, :], start=(kc == 0), stop=(kc == 7)),
                     reads=["screp%d" % w, wb.name], writes=[pm.name])
            dst = modc if w == 0 else modx
            P.op("dve", lambda e, pm=pm, dst=dst, blk=blk: e.tensor_tensor(out=dst[:, blk * 512:(blk + 1) * 512], in0=pm[:], in1=adab[:, blk * 512:(blk + 1) * 512], op=ALU.add),
                 reads=[pm.name, "adab"], writes=[dst.name + str(blk)])
    allc = ["modc%d" % b for b in range(12)]
    allx = ["modx%d" % b for b in range(4)]

    def rep_load(row, tag):
        P.dma("sp", lambda e: e.dma_start(out=grep[:], in_=row.to_broadcast([128, D])), "D_grep", writes=["grep"])
    rep_load(g1, "g1")
    P.op("dve", lambda e: e.scalar_tensor_tensor(out=A1[:], in0=modc[:, D:2 * D], scalar=1.0, in1=grep[:], op0=ALU.add, op1=ALU.mult), reads=allc + ["grep"], writes=["A1"])
    P.op("dve", lambda e: e.scalar_tensor_tensor(out=A1c[:], in0=modx[:, D:2 * D], scalar=1.0, in1=grep[:], op0=ALU.add, op1=ALU.mult), reads=allx + ["grep"], writes=["A1c"])
    P.op("pool", lambda e: e.tensor_copy(out=B1[:], in_=modc[:, 0:D]), reads=allc, writes=["B1"])
    P.op("pool", lambda e: e.tensor_copy(out=B1c[:], in_=modx[:, 0:D]), reads=allx, writes=["B1c"])
    P.op("pool", lambda e: e.tensor_copy(out=G1[:], in_=modc[:, 2 * D:3 * D]), reads=allc, writes=["G1"])
    P.op("pool", lambda e: e.tensor_copy(out=B2[:], in_=modc[:, 3 * D:4 * D]), reads=allc, writes=["B2"])
    P.op("pool", lambda e: e.tensor_copy(out=G2[:], in_=modc[:, 5 * D:6 * D]), reads=allc, writes=["G2"])
    rep_load(g2, "g2")
    P.op("dve", lambda e: e.scalar_tensor_tensor(out=A2[:], in0=modc[:, 4 * D:5 * D], scalar=1.0, in1=grep[:], op0=ALU.add, op1=ALU.mult), reads=allc + ["grep"], writes=["A2"])
    P.dma("sp", lambda e: e.dma_start(out=FG[:], in_=fg.to_broadcast([128, D])), "D_FG", writes=["FG"])
    P.pop()

    P.push()
    xt = [P.sb("xt%d" % i, [128, D]) for i in range(2)]
    pt = [P.sb("pt%d" % i, [128, D]) for i in range(2)]
    junks = [P.sb("junk%d" % i, [128, D]) for i in range(2)]
    hns = [P.sb("hn%d" % i, [128, D]) for i in range(2)]; hnbs = [P.sb("hnb%d" % i, [128, D], BF16) for i in range(2)]
    ssqs = [P.sb("ssq%d" % i, [128, 1]) for i in range(2)]; rstds = [P.sb("rstd%d" % i, [128, 1]) for i in range(2)]
    hT = [P.sb("hT%d" % i, [128, 8, 128], BF16) for i in range(2)]
    psts = [P.ps("pst%d" % i, [128, D], BF16) for i in range(2)]
    hnT_v = hnT_d.rearrange("(kc p) t -> p kc t", p=128)

    def norm_mod_T(src, Arep, Brep, dstT, par):
        junk = junks[par]; hn = hns[par]; hnb = hnbs[par]; ssq = ssqs[par]; rstd = rstds[par]; pst = psts[par]
        P.op("act", lambda e: e.activation(out=junk[:], in_=src[:], func=AF.Square, accum_out=ssq[:]), reads=[src.name], writes=[junk.name, ssq.name])
        P.op("act", lambda e: e.activation(out=rstd[:], in_=ssq[:], func=AF.Sqrt, scale=1.0 / D, bias=EPS), reads=[ssq.name], writes=[rstd.name])
        P.op("dve", lambda e: e.reciprocal(out=rstd[:], in_=rstd[:]), reads=[rstd.name], writes=[rstd.name])
        P.op("dve", lambda e: e.scalar_tensor_tensor(out=hn[:], in0=src[:], scalar=rstd[:, 0:1], in1=Arep[:], op0=ALU.mult, op1=ALU.mult), reads=[src.name, rstd.name, Arep.name], writes=[hn.name])
        P.op("pool", lambda e: e.tensor_tensor(out=hnb[:], in0=hn[:], in1=Brep[:], op=ALU.add), reads=[hn.name, Brep.name], writes=[hnb.name])
        for kc in range(8):
            P.op("pe", lambda e, kc=kc: e.transpose(out=pst[:, kc * 128:(kc + 1) * 128], in_=hnb[:, kc * 128:(kc + 1) * 128], identity=ident[:]), reads=[hnb.name, "ident"], writes=[pst.name])
        P.op("act", lambda e: e.copy(out=dstT[:].rearrange("p k t -> p (k t)"), in_=pst[:]), reads=[pst.name], writes=[dstT.name])

    for tt in range(34):
        xb = xt[tt % 2]; pb = pt[tt % 2]; hb = hT[tt % 2]
        if tt < 32:
            P.dma("sp", lambda e, xb=xb, tt=tt: e.dma_start(out=xb[:], in_=x[tt * 128:(tt + 1) * 128, :]), "D_" + xb.name, writes=[xb.name])
            P.dma("sp", lambda e, pb=pb, tt=tt: e.dma_start(out=pb[:], in_=pos[tt * 128:(tt + 1) * 128, :]), "D_" + pb.name, writes=[pb.name])
            P.op("pool", lambda e, xb=xb, pb=pb: e.tensor_tensor(out=xb[:], in0=xb[:], in1=pb[:], op=ALU.add), reads=[xb.name, pb.name], writes=[xb.name])
            norm_mod_T(xb, A1, B1, hb, tt % 2)
        else:
            c0 = (tt - 32) * 128
            P.dma("sp", lambda e, xb=xb, c0=c0: e.dma_start(out=xb[:], in_=ctx[c0:c0 + 128, :]), "D_" + xb.name, writes=[xb.name])
            norm_mod_T(xb, A1c, B1c, hb, tt % 2)
        P.dma("act", lambda e, hb=hb, tt=tt: e.dma_start(out=hnT_v[:, :, tt * 128:(tt + 1) * 128], in_=hb[:]), "DS_" + hb.name, reads=[hb.name], writes=["hnT_d%d" % tt])
    P.pop()
    P.pop()

    if int(os.environ.get('PH_STOP', '99')) < 1:
        P.barrier(); P.emit(); return nc
    P.push()
    uT = P.sb("uT", [128, 4, LT], BF16)
    P.push()
    winb = P.sb("winb", [128, 8, 2048], BF16)
    hblk = [P.sb("hblk%d" % i, [128, 8, 512], BF16) for i in range(2)]
    zb = [P.sb("zb%d" % i, [128, 512]) for i in range(2)]
    psp = [P.ps("psp%d" % i, [128, 512]) for i in range(2)]
    P.dma("pool", lambda e: e.dma_start(out=winb[:], in_=w_in.rearrange("(kc p) n -> p kc n", p=128)), "D_winb", writes=["winb"])
    k = 0
    for tb in range(9):
        hb = hblk[tb % 2]
        nt = 512 if tb < 8 else 256
        c0 = tb * 512
        P.dma("sp", lambda e, hb=hb, c0=c0, nt=nt: e.dma_start(out=hb[:, :, 0:nt], in_=hnT_v[:, :, c0:c0 + nt]), "D_" + hb.name,
              reads=["hnT_d%d" % t for t in range(tb * 4, min(34, tb * 4 + 4))], writes=[hb.name])
        for ft in range(16 if tb < 8 else 4):
            pp = psp[k % 2]; k += 1
            for kc in range(8):
                P.op("pe", lambda e, pp=pp, hb=hb, kc=kc, ft=ft, nt=nt: e.matmul(pp[:, 0:nt], lhsT=winb[:, kc, ft * 128:(ft + 1) * 128], rhs=hb[:, kc, 0:nt], start=(kc == 0), stop=(kc == 7)),
                     reads=["winb", hb.name], writes=[pp.name])
            if ft < 4:
                P.op("act", lambda e, pp=pp, ft=ft, c0=c0, nt=nt: e.copy(out=uT[:, ft, c0:c0 + nt], in_=pp[:, 0:nt]), reads=[pp.name], writes=["uT"])
            else:
                z = zb[ft % 2]
                P.op("dve", lambda e, pp=pp, z=z: e.tensor_copy(out=z[:], in_=pp[:]), reads=[pp.name], writes=[z.name])
                P.dma("act", lambda e, z=z, ft=ft, c0=c0: e.dma_start(out=z_d[(ft - 4) * 128:(ft - 3) * 128, c0:c0 + 512], in_=z[:]), "DS_" + z.name, reads=[z.name], writes=["z_d"])
    P.pop()

    if int(os.environ.get('PH_STOP', '99')) < 2:
        P.barrier(); P.emit(); return nc
    P.push()
    ybuf = P.sb("ybuf", [128, 4, L], BF16)
    P.push()
    bpb = P.sb("bpb", [128, 32, 2, 128], BF16)
    cpad = P.sb("cpad", [128, 32, 2, 128], BF16)
    NB = 64
    braw = [P.sb("braw%d" % i, [128, 16, 2, NB], BF16) for i in range(4)]
    Ck = P.sb("Ck", [128, 16, NB], BF16); Sk = P.sb("Sk", [128, 16, NB], BF16)
    D0 = P.sb("D0", [128, 16, NB])
    Tm1 = P.sb("Tm1", [128, 16, NB], BF16); Tm2 = P.sb("Tm2", [128, 16, NB], BF16)
    Tm3 = P.sb("Tm3", [128, 16, NB], BF16); Tm4 = P.sb("Tm4", [128, 16, NB], BF16)
    itm2 = P.sb("itm2", [128, 16])
    prr = P.sb("prr", [128, 16, NB], BF16); pri = P.sb("pri", [128, 16, NB], BF16)
    rrs = [P.sb("rr%d" % i, [128, 16, NB], BF16) for i in range(2)]; rims = [P.sb("rim%d" % i, [128, 16, NB], BF16) for i in range(2)]
    Tp1 = P.sb("Tp1", [128, 16, NB], BF16); Tp2 = P.sb("Tp2", [128, 16, NB], BF16)
    Tp3 = P.sb("Tp3", [128, 16, NB], BF16); Tp4 = P.sb("Tp4", [128, 16, NB], BF16)
    st = [P.sb("st%d" % i, [128, 16, 2, 2 * NB], BF16) for i in range(2)]
    cNr = P.sb("cNr", [128, 16]); cNi = P.sb("cNi", [128, 16]); inj = P.sb("inj", [128, 2, 16]); itmp = P.sb("itmp", [128, 16])
    wr = P.sb("wr", [128, 16]); wi = P.sb("wi", [128, 16])
    Aco = P.sb("Aco", [128, 2, 16, 2])
    d5 = P.sb("d5", [128, 4])
    pr = P.sb("pr", [128, 3, 16])
    cl = P.sb("cl", [128, 2, 16, 16])
    tsm = [P.sb("tsm%d" % i, [128, 16]) for i in range(12)]
    cpr = P.sb("cpr", [128, 16, 16]); cpi = P.sb("cpi", [128, 16, 16]); ctmp = P.sb("ctmp", [128, 16, 16])
    ytmps = [P.sb("ytmp%d" % i, [128, 128]) for i in range(2)]
    psb = [P.ps("psb%d" % i, [128, 2, 256]) for i in range(2)]
    psy = [P.ps("psy%d" % i, [128, 512]) for i in range(4)]
    P.dma("sp", lambda e: e.dma_start(out=d5[:], in_=s5d), "D_d5", writes=["d5"])

    def small(eng, fn, reads, writes):
        P.op(eng, fn, reads=reads, writes=writes)

    for pas in range(2):
        dr = 1 - pas
        P.dma("pool", lambda e, dr=dr: e.dma_start(out=bpb[:].rearrange("p g r m -> p (g r m)"), in_=bpad[dr]), "D_bpb", writes=["bpb"])
        P.dma("sp", lambda e, dr=dr: e.dma_start(out=pr[:], in_=s5p[dr].rearrange("k p g -> p k g")), "D_pr", writes=["pr"])
        P.dma("sp", lambda e, dr=dr: e.dma_start(out=cl[:].rearrange("p r g c -> p r (g c)"), in_=clay[dr].rearrange("r p m -> p r m")), "D_cl", writes=["cl"])
        are = pr[:, 0, :]; aim = pr[:, 1, :]; lst = pr[:, 2, :]
        dt_, mag, ang, cs, sn, t1, t2, t3, lr, li, kr, ki = [t[:] for t in tsm]
        T = ["tsm"]
        small("act", lambda e: e.activation(out=dt_, in_=lst, func=AF.Exp), ["pr"], T)
        small("dve", lambda e: e.tensor_tensor(out=mag, in0=are, in1=dt_, op=ALU.mult), ["pr"] + T, T)
        small("act", lambda e: e.activation(out=mag, in_=mag, func=AF.Exp), T, T)
        small("dve", lambda e: e.scalar_tensor_tensor(out=ang, in0=aim, scalar=1.0 / 16, in1=dt_, op0=ALU.mult, op1=ALU.mult), ["pr"] + T, T)
        small("act", lambda e: e.activation(out=sn, in_=ang, func=AF.Sin), T, T)
        small("act", lambda e: e.activation(out=cs, in_=ang, func=AF.Sin, bias=math.pi / 2), T, T)
        for _ in range(4):
            small("dve", lambda e: e.tensor_tensor(out=t1, in0=cs, in1=cs, op=ALU.mult), T, T)
            small("dve", lambda e: e.tensor_tensor(out=t2, in0=sn, in1=sn, op=ALU.mult), T, T)
            small("dve", lambda e: e.tensor_tensor(out=t3, in0=cs, in1=sn, op=ALU.mult), T, T)
            small("dve", lambda e: e.tensor_tensor(out=cs, in0=t1, in1=t2, op=ALU.subtract), T, T)
            small("dve", lambda e: e.tensor_scalar(out=sn, in0=t3, scalar1=2.0, scalar2=None, op0=ALU.mult), T, T)
        small("dve", lambda e: e.tensor_tensor(out=lr, in0=mag, in1=cs, op=ALU.mult), T, T)
        small("dve", lambda e: e.tensor_tensor(out=li, in0=mag, in1=sn, op=ALU.mult), T, T)
        small("dve", lambda e: e.tensor_tensor(out=t1, in0=are, in1=are, op=ALU.mult), ["pr"] + T, T)
        small("dve", lambda e: e.tensor_tensor(out=t2, in0=aim, in1=aim, op=ALU.mult), ["pr"] + T, T)
        small("dve", lambda e: e.tensor_tensor(out=t1, in0=t1, in1=t2, op=ALU.add), T, T)
        small("dve", lambda e: e.reciprocal(out=t1, in_=t1), T, T)
        small("dve", lambda e: e.tensor_scalar(out=t2, in0=lr, scalar1=-1.0, scalar2=None, op0=ALU.add), T, T)
        small("dve", lambda e: e.tensor_tensor(out=kr, in0=t2, in1=are, op=ALU.mult), ["pr"] + T, T)
        small("dve", lambda e: e.tensor_tensor(out=t3, in0=li, in1=aim, op=ALU.mult), ["pr"] + T, T)
        small("dve", lambda e: e.tensor_tensor(out=kr, in0=kr, in1=t3, op=ALU.add), T, T)
        small("dve", lambda e: e.tensor_tensor(out=kr, in0=kr, in1=t1, op=ALU.mult), T, T)
        small("dve", lambda e: e.tensor_tensor(out=ki, in0=li, in1=are, op=ALU.mult), ["pr"] + T, T)
        small("dve", lambda e: e.tensor_tensor(out=t3, in0=t2, in1=aim, op=ALU.mult), ["pr"] + T, T)
        small("dve", lambda e: e.tensor_tensor(out=ki, in0=ki, in1=t3, op=ALU.subtract), T, T)
        small("dve", lambda e: e.tensor_tensor(out=ki, in0=ki, in1=t1, op=ALU.mult), T, T)
        small("dve", lambda e: e.tensor_copy(out=Aco[:, 0, :, 0], in_=lr), T, ["Aco"])
        small("dve", lambda e: e.tensor_copy(out=Aco[:, 0, :, 1], in_=lr), T, ["Aco"])
        small("dve", lambda e: e.tensor_scalar(out=Aco[:, 1, :, 0], in0=li, scalar1=-1.0, scalar2=None, op0=ALU.mult), T, ["Aco"])
        small("dve", lambda e: e.tensor_copy(out=Aco[:, 1, :, 1], in_=li), T, ["Aco"])
        krb = tsm[10][:].unsqueeze(2).to_broadcast([128, 16, 16]); kib = tsm[11][:].unsqueeze(2).to_broadcast([128, 16, 16])
        small("dve", lambda e: e.tensor_tensor(out=cpr[:], in0=cl[:, 0], in1=krb, op=ALU.mult), ["cl"] + T, ["cpr"])
        small("dve", lambda e: e.tensor_tensor(out=ctmp[:], in0=cl[:, 1], in1=kib, op=ALU.mult), ["cl"] + T, ["ctmp"])
        small("dve", lambda e: e.tensor_tensor(out=cpr[:], in0=cpr[:], in1=ctmp[:], op=ALU.subtract), ["cpr", "ctmp"], ["cpr"])
        small("dve", lambda e: e.tensor_tensor(out=cpi[:], in0=cl[:, 0], in1=kib, op=ALU.mult), ["cl"] + T, ["cpi"])
        small("dve", lambda e: e.tensor_tensor(out=ctmp[:], in0=cl[:, 1], in1=krb, op=ALU.mult), ["cl", "cpr"] + T, ["ctmp"])
        small("dve", lambda e: e.tensor_tensor(out=cpi[:], in0=cpi[:], in1=ctmp[:], op=ALU.add), ["cpi", "ctmp"], ["cpi"])
        small("dve", lambda e: e.tensor_scalar(out=cpi[:], in0=cpi[:], scalar1=-1.0, scalar2=None, op0=ALU.mult), ["cpi"], ["cpi"])
        small("pool", lambda e: e.memset(cpad[:], 0.0), [], ["cpad"])
        for gh in range(2):
            for g16 in range(16):
                g = gh * 16 + g16
                for ri, src in ((0, cpr), (1, cpi)):
                    small("pool", lambda e, gh=gh, g16=g16, g=g, ri=ri, src=src: e.tensor_copy(
                        out=cpad[gh * 64:(gh + 1) * 64, g, ri, (g % 8) * 16:(g % 8) * 16 + 16], in_=src[gh * 64:(gh + 1) * 64, g16, :]),
                        ["cpr", "cpi"], ["cpad"])
        small("dve", lambda e: e.tensor_copy(out=wr[:], in_=cs), T, ["wri"])
        small("dve", lambda e: e.tensor_copy(out=wi[:], in_=sn), T, ["wri"])
        rev = (dr == 1)
        i0_ = NB - 1 if rev else 0
        P.push()
        Ckf = P.sb("Ckf%d" % pas, [128, 16, NB]); Skf = P.sb("Skf%d" % pas, [128, 16, NB])
        T1 = P.sb("T1_%d" % pas, [128, 16, NB // 2]); T2 = P.sb("T2_%d" % pas, [128, 16, NB // 2])
        small("pool", lambda e, i0_=i0_: e.memset(Ckf[:, :, i0_:i0_ + 1], 1.0), [], ["tab"])
        small("pool", lambda e, i0_=i0_: e.memset(Skf[:, :, i0_:i0_ + 1], 0.0), [], ["tab"])
        m = 1
        while m < NB:
            wrb = wr[:].unsqueeze(2).to_broadcast([128, 16, m]); wib = wi[:].unsqueeze(2).to_broadcast([128, 16, m])
            src = slice(NB - m, NB) if rev else slice(0, m)
            dst = slice(NB - 2 * m, NB - m) if rev else slice(m, 2 * m)
            small("dve", lambda e, m=m, wrb=wrb, src=src: e.tensor_tensor(out=T1[:, :, 0:m], in0=Ckf[:, :, src], in1=wrb, op=ALU.mult), ["tab", "wri"], ["T1"])
            small("pool", lambda e, m=m, wib=wib, src=src: e.tensor_tensor(out=T2[:, :, 0:m], in0=Skf[:, :, src], in1=wib, op=ALU.mult), ["tab", "wri"], ["T2"])
            small("dve", lambda e, m=m, dst=dst: e.tensor_tensor(out=Ckf[:, :, dst], in0=T1[:, :, 0:m], in1=T2[:, :, 0:m], op=ALU.subtract), ["T1", "T2"], ["tab"])
            small("dve", lambda e, m=m, wib=wib, src=src: e.tensor_tensor(out=T1[:, :, 0:m], in0=Ckf[:, :, src], in1=wib, op=ALU.mult), ["tab", "wri"], ["T1"])
            small("pool", lambda e, m=m, wrb=wrb, src=src: e.tensor_tensor(out=T2[:, :, 0:m], in0=Skf[:, :, src], in1=wrb, op=ALU.mult), ["tab", "wri"], ["T2"])
            small("dve", lambda e, m=m, dst=dst: e.tensor_tensor(out=Skf[:, :, dst], in0=T1[:, :, 0:m], in1=T2[:, :, 0:m], op=ALU.add), ["T1", "T2"], ["tab"])
            small("dve", lambda e: e.tensor_tensor(out=t1, in0=wr[:], in1=wr[:], op=ALU.mult), ["wri"] + T, T)
            small("dve", lambda e: e.tensor_tensor(out=t2, in0=wi[:], in1=wi[:], op=ALU.mult), ["wri"] + T, T)
            small("dve", lambda e: e.tensor_tensor(out=t3, in0=wr[:], in1=wi[:], op=ALU.mult), ["wri"] + T, T)
            small("dve", lambda e: e.tensor_tensor(out=wr[:], in0=t1, in1=t2, op=ALU.subtract), T, ["wri"])
            small("dve", lambda e: e.tensor_scalar(out=wi[:], in0=t3, scalar1=2.0, scalar2=None, op0=ALU.mult), T, ["wri"])
            m *= 2
        small("act", lambda e: e.copy(out=Ck[:], in_=Ckf[:]), ["tab"], ["tabb"])
        small("act", lambda e: e.copy(out=Sk[:], in_=Skf[:]), ["tab"], ["tabb"])
        P.pop()
        small("dve", lambda e: e.tensor_tensor(out=cNr[:], in0=wr[:], in1=mag, op=ALU.mult), ["wri"] + T, ["cN"])
        small("dve", lambda e: e.tensor_tensor(out=cNi[:], in0=wi[:], in1=mag, op=ALU.mult), ["wri"] + T, ["cN"])
        small("dve", lambda e: e.tensor_copy(out=D0[:], in_=mag.unsqueeze(2).to_broadcast([128, 16, NB])), T, ["D0"])
        small("dve", lambda e, i0_=i0_: e.memset(D0[:, :, i0_:i0_ + 1], 0.0), ["D0"], ["D0"])
        small("pool", lambda e: e.memset(inj[:], 0.0), [], ["inj"])
        cblocks = [(L + b * NB, False) for b in range(LC // NB)]
        lblocks = [(b * NB, True) for b in range(L // NB)]
        blocks = (cblocks + lblocks) if dr == 0 else (cblocks[::-1] + lblocks[::-1])
        jf = NB - 1 if rev else 0
        jl = 0 if rev else NB - 1
        fl = (lambda ap: ap.rearrange("p g t -> p (g t)")[:, ::-1]) if rev else (lambda ap: ap.rearrange("p g t -> p (g t)"))
        def pair_base(p_):
            return min(blocks[2 * p_][0], blocks[2 * p_ + 1][0])

        def emit_BU2(p_):
            base = pair_base(p_)
            for g16 in range(16):
                pb_ = psb[g16 % 2]
                for ri in range(2):
                    P.op("pe", lambda e, pb_=pb_, g16=g16, ri=ri: e.matmul(pb_[:, ri, 0:2 * NB], lhsT=bpb[:, g16, ri, :], rhs=uT[:, g16 // 8, base:base + 2 * NB], start=True, stop=False),
                         reads=["bpb", "uT"], writes=[pb_.name])
                    P.op("pe", lambda e, pb_=pb_, g16=g16, ri=ri: e.matmul(pb_[:, ri, 0:2 * NB], lhsT=bpb[:, 16 + g16, ri, :], rhs=uT[:, 2 + g16 // 8, base:base + 2 * NB], start=False, stop=True),
                         reads=["bpb", "uT"], writes=[pb_.name])
                for b_ in (2 * p_, 2 * p_ + 1):
                    br = braw[b_ % 4]; off = blocks[b_][0] - base
                    P.op("act", lambda e, pb_=pb_, g16=g16, br=br, off=off: e.copy(out=br[:, g16, :, :], in_=pb_[:, :, off:off + NB]), reads=[pb_.name], writes=[br.name])

        def emit_scan(bi, jf=jf, jl=jl, fl=fl):
            br = braw[bi % 4]; rr = rrs[bi % 2]; rim = rims[bi % 2]
            bre = br[:, :, 0, :]; bim = br[:, :, 1, :]
            P.op("dve", lambda e: e.tensor_tensor(out=Tm1[:], in0=bre, in1=Ck[:], op=ALU.mult), reads=[br.name, "tabb"], writes=["Tm1"])
            P.op("dve", lambda e: e.tensor_tensor(out=Tm2[:], in0=bim, in1=Sk[:], op=ALU.mult), reads=[br.name, "tabb"], writes=["Tm2"])
            P.op("dve", lambda e: e.tensor_tensor(out=Tm3[:], in0=bre, in1=Sk[:], op=ALU.mult), reads=[br.name, "tabb"], writes=["Tm3"])
            P.op("dve", lambda e: e.tensor_tensor(out=Tm4[:], in0=bim, in1=Ck[:], op=ALU.mult), reads=[br.name, "tabb"], writes=["Tm4"])
            P.op("dve", lambda e: e.tensor_tensor(out=prr[:], in0=Tm1[:], in1=Tm2[:], op=ALU.add), reads=["Tm1", "Tm2"], writes=["prr"])
            P.op("dve", lambda e: e.tensor_tensor(out=pri[:], in0=Tm4[:], in1=Tm3[:], op=ALU.subtract), reads=["Tm3", "Tm4"], writes=["pri"])
            P.op("dve", lambda e: e.tensor_tensor(out=prr[:, :, jf], in0=prr[:, :, jf], in1=inj[:, 0, :], op=ALU.add), reads=["prr", "inj"], writes=["prr"])
            P.op("dve", lambda e: e.tensor_tensor(out=pri[:, :, jf], in0=pri[:, :, jf], in1=inj[:, 1, :], op=ALU.add), reads=["pri", "inj"], writes=["pri"])
            P.op("dve", lambda e: e.tensor_tensor_scan(out=fl(rr[:]), data0=fl(D0[:]), data1=fl(prr[:]), initial=0.0, op0=ALU.mult, op1=ALU.add), reads=["prr", "D0"], writes=[rr.name])
            P.op("dve", lambda e: e.tensor_tensor_scan(out=fl(rim[:]), data0=fl(D0[:]), data1=fl(pri[:]), initial=0.0, op0=ALU.mult, op1=ALU.add), reads=["pri", "D0"], writes=[rim.name])
            P.op("dve", lambda e: e.tensor_tensor(out=inj[:, 0, :], in0=rr[:, :, jl], in1=cNr[:], op=ALU.mult), reads=[rr.name, "cN", "prr", "pri"], writes=["inj"])
            P.op("dve", lambda e: e.tensor_tensor(out=itmp[:], in0=rim[:, :, jl], in1=cNi[:], op=ALU.mult), reads=[rim.name, "cN"], writes=["itmp"])
            P.op("dve", lambda e: e.tensor_tensor(out=itm2[:], in0=rr[:, :, jl], in1=cNi[:], op=ALU.mult), reads=[rr.name, "cN"], writes=["itm2"])
            P.op("dve", lambda e: e.tensor_tensor(out=inj[:, 1, :], in0=rim[:, :, jl], in1=cNr[:], op=ALU.mult), reads=[rim.name, "cN", "inj"], writes=["inj"])
            P.op("dve", lambda e: e.tensor_tensor(out=inj[:, 0, :], in0=inj[:, 0, :], in1=itmp[:], op=ALU.subtract), reads=["inj", "itmp"], writes=["inj"])
            P.op("dve", lambda e: e.tensor_tensor(out=inj[:, 1, :], in0=inj[:, 1, :], in1=itm2[:], op=ALU.add), reads=["inj", "itm2"], writes=["inj"])

        def emit_post_mults(bi):
            rr = rrs[bi % 2]; rim = rims[bi % 2]
            P.op("pool", lambda e: e.tensor_tensor(out=Tp1[:], in0=rr[:], in1=Ck[:], op=ALU.mult), reads=[rr.name, "tabb"], writes=["Tp1"])
            P.op("pool", lambda e: e.tensor_tensor(out=Tp2[:], in0=rim[:], in1=Sk[:], op=ALU.mult), reads=[rim.name, "tabb"], writes=["Tp2"])
            P.op("pool", lambda e: e.tensor_tensor(out=Tp3[:], in0=rr[:], in1=Sk[:], op=ALU.mult), reads=[rr.name, "tabb"], writes=["Tp3"])
            P.op("pool", lambda e: e.tensor_tensor(out=Tp4[:], in0=rim[:], in1=Ck[:], op=ALU.mult), reads=[rim.name, "tabb"], writes=["Tp4"])

        def emit_combine(bi):
            p_ = bi // 2
            sb_ = st[p_ % 2]; off = blocks[bi][0] - pair_base(p_)
            P.op("dve", lambda e: e.tensor_tensor(out=sb_[:, :, 0, off:off + NB], in0=Tp1[:], in1=Tp2[:], op=ALU.subtract), reads=["Tp1", "Tp2"], writes=[sb_.name])
            P.op("dve", lambda e: e.tensor_tensor(out=sb_[:, :, 1, off:off + NB], in0=Tp3[:], in1=Tp4[:], op=ALU.add), reads=["Tp3", "Tp4"], writes=[sb_.name])

        def emit_cproj(p_):
            sb_ = st[p_ % 2]
            for ot in range(4):
                py = psy[ot]; pyn = "psy%d" % ot
                n = 0
                for g in range(ot * 8, ot * 8 + 8):
                    for ri in range(2):
                        P.op("pe", lambda e, py=py, g=g, ri=ri, n=n: e.matmul(py[:, 0:2 * NB], lhsT=cpad[:, g, ri, :], rhs=sb_[:, g % 16, ri, :], start=(n == 0), stop=(n == 15)),
                             reads=["cpad", sb_.name], writes=[pyn])
                        n += 1

        def emit_y(p_, pas=pas):
            c0 = pair_base(p_); W = 2 * NB
            for ot in range(4):
                py = psy[ot]; pyn = "psy%d" % ot
                if pas == 0:
                    P.op("act", lambda e, py=py, ot=ot: e.copy(out=ybuf[:, ot, c0:c0 + W], in_=py[:, 0:W]), reads=[pyn], writes=["ybuf"])
                else:
                    yt_ = ytmps[ot % 2]
                    P.op("dve", lambda e, py=py, ot=ot, yt_=yt_: e.tensor_tensor(out=yt_[:, 0:W], in0=py[:, 0:W], in1=ybuf[:, ot, c0:c0 + W], op=ALU.add), reads=[pyn, "ybuf"], writes=[yt_.name])
                    P.op("dve", lambda e, ot=ot, yt_=yt_: e.scalar_tensor_tensor(out=yt_[:, 0:W], in0=uT[:, ot, c0:c0 + W], scalar=d5[:, ot:ot + 1], in1=yt_[:, 0:W], op0=ALU.mult, op1=ALU.add), reads=["uT", "d5", yt_.name], writes=[yt_.name])
                    P.op("act", lambda e, ot=ot, yt_=yt_: e.activation(out=ybuf[:, ot, c0:c0 + W], in_=yt_[:, 0:W], func=AF.Gelu_apprx_tanh), reads=[yt_.name], writes=["ybuf"])

        nblk = len(blocks)
        emit_BU2(0)
        pend_comb = None
        pend_y = None
        for bi, (c0, islat) in enumerate(blocks):
            if bi % 2 == 0 and bi + 2 < nblk:
                emit_BU2(bi // 2 + 1)
            emit_scan(bi)
            if pend_y is not None:
                emit_y(pend_y)
                pend_y = None
            if pend_comb is not None:
                emit_combine(pend_comb)
                if pend_comb % 2 == 1:
                    emit_cproj(pend_comb // 2)
                    pend_y = pend_comb // 2
                pend_comb = None
            if islat:
                emit_post_mults(bi)
                pend_comb = bi
        if pend_y is not None:
            emit_y(pend_y)
            pend_y = None
        if pend_comb is not None:
            emit_combine(pend_comb)
            emit_cproj(pend_comb // 2)
            emit_y(pend_comb // 2)
    P.pop()
    if int(os.environ.get('PH_STOP', '99')) < 3:
        P.barrier(); P.emit(); return nc
    gwb = P.sb("gwb", [128, 4, 1024], BF16)
    gb = P.sb("gb", [128, 8])
    sig = P.sb("sig", [128, 512]); ym = [P.sb("ym%d" % i, [128, 512], BF16) for i in range(2)]
    ysq = P.sb("ysq", [128, 512], BF16)
    psg = [P.ps("psg%d" % i, [128, 512]) for i in range(2)]
    pss = P.ps("pss", [128, 4, 32])
    P.dma("pool", lambda e: e.dma_start(out=gwb[:], in_=glu_w.rearrange("(kc p) n -> p kc n", p=128)), "D_gwb", writes=["gwb"])
    P.dma("sp", lambda e: e.dma_start(out=gb[:], in_=glu_b), "D_gb", writes=["gb"])
    for tb in range(8):
        c0 = tb * 512
        for et in range(4):
            for j, col in enumerate((et, et + 4)):
                for kc in range(4):
                    P.op("pe", lambda e, j=j, col=col, kc=kc, c0=c0: e.matmul(psg[j][:], lhsT=gwb[:, kc, col * 128:(col + 1) * 128], rhs=ybuf[:, kc, c0:c0 + 512], start=(kc == 0), stop=(kc == 3)),
                         reads=["gwb", "ybuf"], writes=[psg[j].name])
            y_ = ym[et % 2]
            P.op("act", lambda e, et=et: e.activation(out=sig[:], in_=psg[1][:], func=AF.Sigmoid, bias=gb[:, et + 4:et + 5]), reads=[psg[1].name, "gb"], writes=["sig"])
            P.op("dve", lambda e, et=et, y_=y_: e.scalar_tensor_tensor(out=y_[:], in0=psg[0][:], scalar=gb[:, et:et + 1], in1=sig[:], op0=ALU.add, op1=ALU.mult), reads=[psg[0].name, "gb", "sig"], writes=[y_.name])
            P.dma("act", lambda e, et=et, c0=c0, y_=y_: e.dma_start(out=ymix_d[et * 128:(et + 1) * 128, c0:c0 + 512], in_=y_[:]), "DS_" + y_.name, reads=[y_.name], writes=["ymix_d"])
            P.op("pool", lambda e, y_=y_: e.tensor_tensor(out=ysq[:], in0=y_[:], in1=y_[:], op=ALU.mult), reads=[y_.name], writes=["ysq"])
            for sub in range(4):
                tt = tb * 4 + sub
                P.op("pe", lambda e, sub=sub, tt=tt, et=et: e.matmul(pss[:, et, tt:tt + 1], lhsT=ysq[:, sub * 128:(sub + 1) * 128], rhs=ones[:], start=True, stop=True),
                     reads=["ysq", "ones"], writes=["pss"])
    P.op("dve", lambda e: e.tensor_copy(out=rs[:], in_=pss[:, 0, :]), reads=["pss"], writes=["rs"])
    P.op("dve", lambda e: e.tensor_tensor(out=rs[:], in0=rs[:], in1=pss[:, 1, :], op=ALU.add), reads=["pss", "rs"], writes=["rs"])
    P.op("dve", lambda e: e.tensor_tensor(out=rs[:], in0=rs[:], in1=pss[:, 2, :], op=ALU.add), reads=["pss", "rs"], writes=["rs"])
    P.op("dve", lambda e: e.tensor_tensor(out=rs[:], in0=rs[:], in1=pss[:, 3, :], op=ALU.add), reads=["pss", "rs"], writes=["rs"])
    P.op("act", lambda e: e.activation(out=rs[:], in_=rs[:], func=AF.Sqrt, scale=1.0 / 512, bias=EPS), reads=["rs"], writes=["rs"])
    P.op("dve", lambda e: e.reciprocal(out=rs[:], in_=rs[:]), reads=["rs"], writes=["rs"])
    P.pop()
    P.pop()

    if int(os.environ.get('PH_STOP', '99')) < 4:
        P.barrier(); P.emit(); return nc
    NBT = 8
    fx_d = dscr("fx_d", [2, 2, NBT, 64, L], BF16)
    xl_d = dscr("xl_d", [3, NBT, 64, L], BF16)
    hq_d = dscr("hq_d", [2, NBT, 128, 128 * 64], BF16)
    yx_d = dscr("yx_d", [NBT, 64, L], BF16)
    P.push()
    fv = P.sb("fv", [64, 3]); fsc = P.sb("fsc", [64, 4])
    w3s = P.sb("w3s", [64, 2048])
    h2 = [P.sb("h2T%d" % i, [64, L], BF16) for i in range(2)]
    w3b = P.sb("w3b", [64, 2048], BF16)
    psf = [P.ps("psf%d" % i, [128, 512]) for i in range(2)]
    P.dma("sp", lambda e: e.dma_start(out=fv[:], in_=fvec), "D_fv", writes=["fv"])
    P.dma("sp", lambda e: e.dma_start(out=w3s[:], in_=fw3), "D_w3s", writes=["w3s"])
    P.op("act", lambda e: e.copy(out=w3b[:], in_=w3s[:]), reads=["w3s"], writes=["w3b"])
    P.op("dve", lambda e: e.tensor_scalar(out=fsc[:, 0:1], in0=fv[:, 2:3], scalar1=1.0 / 3, scalar2=None, op0=ALU.mult), reads=["fv"], writes=["fsc"])
    P.op("dve", lambda e: e.tensor_tensor(out=fsc[:, 1:3], in0=fv[:, 0:2], in1=fsc[:, 0:1].to_broadcast([64, 2]), op=ALU.mult), reads=["fv", "fsc"], writes=["fsc"])
    P.push()
    emb = P.sb("emb", [33, L]); w1s = P.sb("w1s", [33, 64]); w2s = P.sb("w2s", [64, 64])
    h1T = P.sb("h1T", [64, L]); stmp = P.sb("stmp", [64, 512]); stm2 = P.sb("stm2", [64, 512])
    P.dma("sp", lambda e: e.dma_start(out=w1s[:], in_=fw1), "D_w1s", writes=["w1s"])
    P.dma("sp", lambda e: e.dma_start(out=w2s[:], in_=fw2), "D_w2s", writes=["w2s"])
    for var in range(2):
        P.dma("sp", lambda e, var=var: e.dma_start(out=emb[:], in_=embT[var]), "D_emb", writes=["emb"])
        for layer, (wsrc, src, dst, bcol) in enumerate(((w1s, emb, h1T, 1), (w2s, h1T, h2[var], 2))):
            for cbk in range(8):
                pf = psf[cbk % 2]
                P.op("pe", lambda e, pf=pf, wsrc=wsrc, src=src, cbk=cbk: e.matmul(pf[0:64, :], lhsT=wsrc[:], rhs=src[:, cbk * 512:(cbk + 1) * 512], start=True, stop=True),
                     reads=[wsrc.name, src.name], writes=[pf.name])
                P.op("act", lambda e, pf=pf, bcol=bcol: e.activation(out=stmp[:], in_=pf[0:64, :], func=AF.Sin, scale=fsc[:, 0:1], bias=fsc[:, bcol:bcol + 1]), reads=[pf.name, "fsc"], writes=["stmp"])
                P.op("dve", lambda e: e.tensor_tensor(out=stm2[:], in0=stmp[:], in1=stmp[:], op=ALU.mult), reads=["stmp"], writes=["stm2"])
                P.op("dve", lambda e: e.tensor_scalar(out=stm2[:], in0=stm2[:], scalar1=-4.0, scalar2=3.0, op0=ALU.mult, op1=ALU.add), reads=["stm2"], writes=["stm2"])
                P.op("dve", lambda e, dst=dst, cbk=cbk: e.tensor_tensor(out=dst[:, cbk * 512:(cbk + 1) * 512], in0=stm2[:], in1=stmp[:], op=ALU.mult), reads=["stm2", "stmp"], writes=[dst.name])
    P.pop()
    tau = P.sb("tau", [64, 2, 64])
    dneg = P.sb("dneg", [64, 2, 512]); brow = P.sb("brow", [1, 2, 512])
    warg = P.sb("warg", [64, 64, 64]); wwin = P.sb("wwin", [64, 64, 64])
    fxt = [P.sb("fxt%d" % i, [64, 64, 64], BF16) for i in range(2)]
    P.dma("sp", lambda e: e.dma_start(out=tau[:], in_=tauX), "D_tau", writes=["tau"])
    P.dma("sp", lambda e: e.dma_start(out=dneg[:].rearrange("p o c -> p (o c)"), in_=hdec_row.to_broadcast([64, 1024])), "D_dneg", writes=["dneg"])
    P.dma("sp", lambda e: e.dma_start(out=brow[:].rearrange("p o c -> p (o c)"), in_=hbias_row), "D_brow", writes=["brow"])
    P.op("act", lambda e: e.activation(out=dneg[:], in_=dneg[:], func=AF.Abs), reads=["dneg"], writes=["dneg"])
    P.op("dve", lambda e: e.tensor_scalar(out=dneg[:], in0=dneg[:], scalar1=-1.0, scalar2=None, op0=ALU.mult), reads=["dneg"], writes=["dneg"])
    kf = 0
    for o in range(2):
        for bt in range(NBT):
            cg = bt * 64
            for d_ in range(2):
                P.op("dve", lambda e, o=o, cg=cg, d_=d_: e.tensor_tensor(out=warg[:], in0=tau[:, d_, :].unsqueeze(2).to_broadcast([64, 64, 64]),
                                                                 in1=dneg[:, o, cg:cg + 64].unsqueeze(1).to_broadcast([64, 64, 64]), op=ALU.mult),
                     reads=["tau", "dneg"], writes=["warg"])
                P.op("act", lambda e: e.activation(out=wwin[:], in_=warg[:], func=AF.Exp), reads=["warg"], writes=["wwin"])
                ft_ = fxt[kf % 2]; kf += 1
                col = o * 1024 + d_ * 512 + cg
                for n8 in range(8):
                    pf = psf[n8 % 2]
                    for j in range(8):
                        n2 = n8 * 8 + j
                        P.op("pe", lambda e, pf=pf, j=j, n2=n2, d_=d_, col=col: e.matmul(pf[0:64, j * 64:(j + 1) * 64], lhsT=h2[d_][:, n2:L:64], rhs=w3b[:, col:col + 64], start=True, stop=True),
                             reads=[h2[d_].name, "w3b"], writes=[pf.name])
                    pv = pf[0:64, :].rearrange("p (j c) -> p j c", j=8)
                    if d_ == 0:
                        P.op("dve", lambda e, pv=pv, ft_=ft_, n8=n8: e.tensor_tensor(out=ft_[:, :, n8 * 8:(n8 + 1) * 8].rearrange("p c n -> p n c"), in0=pv, in1=wwin[:, n8 * 8:(n8 + 1) * 8, :], op=ALU.mult),
                             reads=[pf.name, "wwin"], writes=[ft_.name])
                    else:
                        P.op("dve", lambda e, pv=pv, ft_=ft_, n8=n8: e.scalar_tensor_tensor(out=ft_[:, :, n8 * 8:(n8 + 1) * 8].rearrange("p c n -> p n c"), in0=pv, scalar=-1.0, in1=wwin[:, n8 * 8:(n8 + 1) * 8, :], op0=ALU.mult, op1=ALU.mult),
                             reads=[pf.name, "wwin"], writes=[ft_.name])
                if d_ == 0:
                    P.op("dve", lambda e, ft_=ft_, o=o, cg=cg: e.tensor_tensor(out=ft_[0:1, :, 0], in0=ft_[0:1, :, 0], in1=brow[0:1, o, cg:cg + 64], op=ALU.add), reads=[ft_.name, "brow"], writes=[ft_.name])
                else:
                    P.op("dve", lambda e, ft_=ft_: e.memset(ft_[0:1, :, 0], 0.0), reads=[ft_.name], writes=[ft_.name])
                P.dma("act", lambda e, ft_=ft_, o=o, d_=d_, bt=bt: e.dma_start(out=fx_d[o, d_, bt], in_=ft_[:].rearrange("p a c -> p (a c)")), "DS_" + ft_.name, reads=[ft_.name], writes=["fx_d"])
    P.pop()
    if STOP < 1:
        P.barrier(); P.emit(); return nc
    P.push()
    cws = P.sb("cws", [128, 12, 3]); cbs = P.sb("cbs", [128, 12])
    zrs = [P.sb("zr%d" % i, [128, L]) for i in range(2)]; scfs = [P.sb("scf%d" % i, [128, L]) for i in range(2)]; scbs = [P.sb("scb%d" % i, [128, L], BF16) for i in range(2)]
    xtl = [P.sb("xtl%d" % i, [64, 2, 64, 64], BF16) for i in range(2)]
    pstx = [P.ps("pstx%d" % i, [64, 8, 128], BF16) for i in range(2)]
    P.dma("sp", lambda e: e.dma_start(out=cws[:], in_=cw), "D_cws", writes=["cws"])
    P.dma("sp", lambda e: e.dma_start(out=cbs[:], in_=cb), "D_cbs", writes=["cbs"])
    for ft in range(12):
        j_, ct = ft // 4, ft % 4
        zr = zrs[ft % 2]; scf = scfs[ft % 2]; scb = scbs[ft % 2]
        P.dma("sp", lambda e, ft=ft, zr=zr: e.dma_start(out=zr[:], in_=z_d[ft * 128:(ft + 1) * 128, :]), "D_" + zr.name, reads=["z_d"], writes=[zr.name])
        P.op("dve", lambda e, ft=ft, zr=zr, scf=scf: e.tensor_scalar(out=scf[:], in0=zr[:], scalar1=cws[:, ft, 1:2], scalar2=cbs[:, ft:ft + 1], op0=ALU.mult, op1=ALU.add), reads=[zr.name, "cws", "cbs"], writes=[scf.name])
        P.op("dve", lambda e, ft=ft, zr=zr, scf=scf: e.scalar_tensor_tensor(out=scf[:, 1:L], in0=zr[:, 0:L - 1], scalar=cws[:, ft, 0:1], in1=scf[:, 1:L], op0=ALU.mult, op1=ALU.add), reads=[zr.name, "cws", scf.name], writes=[scf.name])
        P.op("dve", lambda e, ft=ft, zr=zr, scf=scf: e.scalar_tensor_tensor(out=scf[:, 0:L - 1], in0=zr[:, 1:L], scalar=cws[:, ft, 2:3], in1=scf[:, 0:L - 1], op0=ALU.mult, op1=ALU.add), reads=[zr.name, "cws", scf.name], writes=[scf.name])
        P.op("act", lambda e, scf=scf, scb=scb: e.copy(out=scb[:], in_=scf[:]), reads=[scf.name], writes=[scb.name])
        xt_ = xtl[ft % 2]
        for n8 in range(8):
            px = pstx[n8 % 2]
            for j in range(8):
                n2 = n8 * 8 + j
                P.op("pe", lambda e, px=px, j=j, n2=n2, scb=scb: e.transpose(out=px[:, j, :], in_=scb[:, n2:L:64], identity=ident[:]), reads=[scb.name, "ident"], writes=[px.name])
            for h in range(2):
                eng = "act" if h == 0 else "dve"
                if eng == "act":
                    P.op("act", lambda e, px=px, xt_=xt_, h=h, n8=n8: e.copy(out=xt_[:, h, :, n8 * 8:(n8 + 1) * 8].rearrange("p c n -> p n c"), in_=px[:, :, h * 64:(h + 1) * 64]), reads=[px.name], writes=[xt_.name])
                else:
                    P.op("dve", lambda e, px=px, xt_=xt_, h=h, n8=n8: e.tensor_copy(out=xt_[:, h, :, n8 * 8:(n8 + 1) * 8].rearrange("p c n -> p n c"), in_=px[:, :, h * 64:(h + 1) * 64]), reads=[px.name], writes=[xt_.name])
        for h in range(2):
            P.dma("act", lambda e, xt_=xt_, j_=j_, ct=ct, h=h: e.dma_start(out=xl_d[j_, ct * 2 + h], in_=xt_[:, h].rearrange("p a c -> p (a c)")), "DS_%s%d" % (xt_.name, h), reads=[xt_.name], writes=["xl_d"])
    P.pop()
    if STOP < 2:
        P.barrier(); P.emit(); return nc
    P.push()
    Gs = P.sb("Gs", [128, 128, 2, 128], BF16)
    GPs = P.sb("GPs", [128, 128, 128], BF16)
    F1s = P.sb("F1s", [64, 2, 256], BF16)
    Es = P.sb("Es", [128, 2, 64], BF16)
    xin = P.sb("xin", [64, 64, 64], BF16); gin = P.sb("gin", [64, 64, 64], BF16); gat = P.sb("gat", [64, 64, 64], BF16)
    AB = P.sb("AB", [128, 16384], BF16)
    Abuf = AB[0:64, :].rearrange("p (r k c) -> p r k c", r=2, k=128)
    Bb = AB[:, 0:8192].rearrange("p (s k) -> p s k", k=128)
    PQ = P.sb("PQ", [128, 8192], BF16)
    Pq = PQ[:].rearrange("p (k s) -> p k s", s=64)
    BT = PQ[:].rearrange("p (r c n) -> p r c n", r=2, c=64)
    hqs = [P.sb("hqs%d" % i, [128, 8, 64], BF16) for i in range(6)]
    ps1 = [P.ps("ps1%d" % i, [64, 2, 256]) for i in range(2)]
    ps2 = [P.ps("ps2%d" % i, [128, 8, 64]) for i in range(2)]
    psB = P.ps("psB", [128, 8, 64])
    psT = P.ps("psT", [128, 8, 128], BF16)
    psA = [P.ps("psA%d" % i, [64, 512]) for i in range(2)]
    P.dma("sp", lambda e: e.dma_start(out=Gs[:].rearrange("p a r m -> p (a r m)"), in_=Gtab), "D_Gs", writes=["Gs"])
    P.dma("sp", lambda e: e.dma_start(out=GPs[:].rearrange("p a m -> p (a m)"), in_=GPtab), "D_GPs", writes=["GPs"])
    P.dma("sp", lambda e: e.dma_start(out=F1s[:].rearrange("p a m -> p (a m)"), in_=F1tab), "D_F1s", writes=["F1s"])
    P.dma("sp", lambda e: e.dma_start(out=Es[:].rearrange("p a m -> p (a m)"), in_=Etab), "D_Es", writes=["Es"])
    ek = [0]

    def evac(fn_act, fn_dve, reads, writes):
        ek[0] += 1
        if ek[0] % 2 == 0:
            P.op("act", fn_act, reads=reads, writes=writes)
        else:
            P.op("dve", fn_dve, reads=reads, writes=writes)

    def stage_F1(srcs):
        for c2_ in range(32):
            p1 = ps1[c2_ % 2]
            for a in range(2):
                c = c2_ * 2 + a
                for si, (tl, var) in enumerate(srcs):
                    P.op("pe", lambda e, p1=p1, a=a, tl=tl, c=c, var=var, si=si: e.matmul(p1[:, a, :], lhsT=tl[:, c, :], rhs=F1s[:, var, :], start=(si == 0), stop=(si == len(srcs) - 1)),
                         reads=[tl.name, "F1s"], writes=[p1.name])
            ov = Abuf[:, :, :, c2_ * 2:c2_ * 2 + 2]
            iv = p1[:].rearrange("p a (r k) -> p r k a", r=2)
            evac(lambda e, ov=ov, iv=iv: e.copy(out=ov, in_=iv), lambda e, ov=ov, iv=iv: e.tensor_copy(out=ov, in_=iv), [p1.name], ["AB"])

    def stage_F2(consume):
        for k8 in range(16):
            p2 = ps2[k8 % 2]
            for kk in range(8):
                k1 = k8 * 8 + kk
                for ri in range(2):
                    P.op("pe", lambda e, p2=p2, kk=kk, k1=k1, ri=ri: e.matmul(p2[:, kk, :], lhsT=Gs[0:64, k1, ri, :], rhs=Abuf[:, ri, k1, :], start=(ri == 0), stop=(ri == 1)),
                         reads=["Gs", "AB"], writes=[p2.name])
            consume(k8, p2)

    MAINSTOP = int(os.environ.get('HY_MAIN', '9999'))
    mstep = [0]

    def chk():
        mstep[0] += 1
        return mstep[0] >= MAINSTOP
    for bt in range(NBT):
        for o in range(2):
            P.dma("sp", lambda e, o=o, bt=bt: e.dma_start(out=xin[:].rearrange("p a c -> p (a c)"), in_=fx_d[o, 0, bt]), "D_xin", reads=["fx_d"], writes=["xin"])
            P.dma("sp", lambda e, o=o, bt=bt: e.dma_start(out=gin[:].rearrange("p a c -> p (a c)"), in_=fx_d[o, 1, bt]), "D_gin", reads=["fx_d"], writes=["gin"])
            stage_F1([(xin, 0), (gin, 1)])

            def cons_h(k8, p2, o=o, bt=bt):
                hq = hqs[k8 % 6]
                iv = p2[:]
                evac(lambda e, hq=hq, iv=iv: e.copy(out=hq[:], in_=iv), lambda e, hq=hq, iv=iv: e.tensor_copy(out=hq[:], in_=iv), [p2.name], [hq.name])
                P.dma("sp", lambda e, hq=hq, k8=k8: e.dma_start(out=hq_d[o, bt, :, k8 * 512:(k8 + 1) * 512], in_=hq[:].rearrange("p k s -> p (k s)")), "DS_" + hq.name, reads=[hq.name], writes=["hq_d%d_%d_%d" % (o, bt, k8)])
            stage_F2(cons_h)
            if chk():
                P.barrier(); P.emit(); return nc
        for o in range(2):
            if o == 0:
                P.dma("sp", lambda e, bt=bt: e.dma_start(out=xin[:].rearrange("p a c -> p (a c)"), in_=xl_d[0, bt]), "D_xin", reads=["xl_d"], writes=["xin"])
            P.dma("sp", lambda e, o=o, bt=bt: e.dma_start(out=gat[:].rearrange("p a c -> p (a c)"), in_=xl_d[1 + o, bt]), "D_gat", reads=["xl_d"], writes=["gat"])
            stage_F1([(xin, 0)])

            def cons_x(k8, p2, o=o, bt=bt):
                hq = hqs[k8 % 6]
                srcv = hq_d[o, bt, :, k8 * 512:(k8 + 1) * 512]
                for (d0, s0, nrow, tg) in ((0, 0, 64, "a"), (64, 96, 32, "b"), (96, 64, 32, "c")):
                    P.dma("sp", lambda e, hq=hq, d0=d0, s0=s0, nrow=nrow, srcv=srcv: e.dma_start(out=hq[d0:d0 + nrow].rearrange("p k s -> p (k s)"), in_=srcv[s0:s0 + nrow, :]),
                          "D_%s%s" % (hq.name, tg), reads=["hq_d%d_%d_%d" % (o, bt, k8)], writes=[hq.name + tg])
                iv = p2[:]
                P.op("dve", lambda e, hq=hq, iv=iv, k8=k8: e.tensor_tensor(out=Pq[:, k8 * 8:(k8 + 1) * 8, :], in0=iv, in1=hq[:], op=ALU.mult),
                     reads=[p2.name, hq.name + "a", hq.name + "b", hq.name + "c"], writes=["PQ", hq.name])
            stage_F2(cons_x)
            if chk():
                P.barrier(); P.emit(); return nc
            for k8 in range(16):
                for kk in range(8):
                    k1 = k8 * 8 + kk
                    P.op("pe", lambda e, kk=kk, k1=k1: e.matmul(psB[:, kk, :], lhsT=GPs[:, k1, :], rhs=Pq[:, k1, :], start=True, stop=True), reads=["GPs", "PQ"], writes=["psB"])
                ov = Bb[:, :, k8 * 8:(k8 + 1) * 8].rearrange("p s k -> p k s")
                evac(lambda e, ov=ov: e.copy(out=ov, in_=psB[:]), lambda e, ov=ov: e.tensor_copy(out=ov, in_=psB[:]), ["psB"], ["AB"])
            if chk():
                P.barrier(); P.emit(); return nc
            for s8 in range(8):
                for j in range(8):
                    sl_ = s8 * 8 + j
                    P.op("pe", lambda e, j=j, sl_=sl_: e.transpose(out=psT[:, j, :], in_=Bb[:, sl_, :], identity=ident[:]), reads=["AB", "ident"], writes=["psT"])
                ov = BT[:, :, s8 * 8:(s8 + 1) * 8, :]
                iv = psT[:].rearrange("p s (r n) -> p r s n", r=2)
                evac(lambda e, ov=ov, iv=iv: e.copy(out=ov, in_=iv), lambda e, ov=ov, iv=iv: e.tensor_copy(out=ov, in_=iv), ["psT"], ["PQ"])
            if chk():
                P.barrier(); P.emit(); return nc
            dst = xin if o == 0 else gin
            for nb in range(8):
                pa = psA[nb % 2]
                for ri in range(2):
                    P.op("pe", lambda e, pa=pa, ri=ri, nb=nb: e.matmul(pa[:], lhsT=Es[:, ri, :], rhs=BT[:, ri, nb * 8:(nb + 1) * 8, :].rearrange("p c n -> p (c n)"), start=(ri == 0), stop=(ri == 1)),
                         reads=["Es", "PQ"], writes=[pa.name])
                P.op("dve", lambda e, pa=pa, dst=dst, nb=nb: e.tensor_tensor(out=dst[:, nb * 8:(nb + 1) * 8, :], in0=pa[:].rearrange("p (c n) -> p c n", c=8), in1=gat[:, nb * 8:(nb + 1) * 8, :], op=ALU.mult),
                     reads=[pa.name, "gat"], writes=[dst.name])
            if chk():
                P.barrier(); P.emit(); return nc
            if o == 1:
                P.dma("sp", lambda e, bt=bt: e.dma_start(out=yx_d[bt], in_=gin[:].rearrange("p a c -> p (a c)")), "DS_gin", reads=["gin"], writes=["yx_d%d" % bt])
    P.pop()
    if STOP < 3:
        P.barrier(); P.emit(); return nc
    P.push()
    Yt = P.sb("Yt", [64, 2, 64, 64], BF16)
    yhb = P.sb("yhb", [128, L], BF16); yhs = P.sb("yhs", [128, 512], BF16)
    psY = [P.ps("psY%d" % i, [128, 16, 64], BF16) for i in range(2)]
    psh = P.ps("psh", [128, 32])
    for ct in range(4):
        for h in range(2):
            P.dma("sp", lambda e, ct=ct, h=h: e.dma_start(out=Yt[:, h].rearrange("p a c -> p (a c)"), in_=yx_d[ct * 2 + h]), "D_Yt%d" % h, reads=["yx_d%d" % (ct * 2 + h)], writes=["Yt%d" % h])
        yv = yhb[:].rearrange("p (a b) -> p a b", b=64)
        for n16 in range(4):
            py = psY[n16 % 2]
            for j in range(16):
                n2 = n16 * 16 + j
                P.op("pe", lambda e, py=py, j=j, n2=n2: e.transpose(out=py[:, j, :], in_=Yt[:, :, :, n2].rearrange("p h c -> p (h c)"), identity=ident[0:64, 0:64]), reads=["Yt0", "Yt1", "ident"], writes=[py.name])
            ov = yv[:, :, n16 * 16:(n16 + 1) * 16]
            iv = py[:].rearrange("p j a -> p a j")
            evac(lambda e, ov=ov, iv=iv: e.copy(out=ov, in_=iv), lambda e, ov=ov, iv=iv: e.tensor_copy(out=ov, in_=iv), [py.name], ["yhb"])
        P.dma("sp", lambda e, ct=ct: e.dma_start(out=ymix_d[512 + ct * 128:512 + (ct + 1) * 128, :], in_=yhb[:]), "DS_yhb", reads=["yhb"], writes=["ymix_d"])
        for tb in range(8):
            P.op("pool", lambda e, tb=tb: e.tensor_tensor(out=yhs[:], in0=yhb[:, tb * 512:(tb + 1) * 512], in1=yhb[:, tb * 512:(tb + 1) * 512], op=ALU.mult), reads=["yhb"], writes=["yhs"])
            for sub in range(4):
                tt = tb * 4 + sub
                P.op("pe", lambda e, sub=sub, tt=tt: e.matmul(psh[:, tt:tt + 1], lhsT=yhs[:, sub * 128:(sub + 1) * 128], rhs=ones[:], start=True, stop=True), reads=["yhs", "ones"], writes=["psh"])
        P.op("dve", lambda e: e.tensor_tensor(out=ssh[:], in0=ssh[:], in1=psh[:], op=ALU.add), reads=["ssh", "psh"], writes=["ssh"])
    P.op("act", lambda e: e.activation(out=rh[:], in_=ssh[:], func=AF.Sqrt, scale=1.0 / 512, bias=EPS), reads=["ssh"], writes=["rh"])
    P.op("dve", lambda e: e.reciprocal(out=rh[:], in_=rh[:]), reads=["rh"], writes=["rh"])
    P.pop()

    if int(os.environ.get('PH_STOP', '99')) < 8:
        P.barrier(); P.emit(); return nc
    P.push()
    wob = P.sb("wob", [128, 8, D], BF16)
    mg = P.sb("mg", [128, 8])
    ymb = P.sb("ymb", [128, 8, 512], BF16)
    wst1 = [P.sb("wst1%d" % i, [128, 8, 1024], BF16) for i in range(2)]
    wst2 = wst1
    hid = P.sb("hid", [128, 32, 512], BF16)
    h1b = P.sb("h1b", [128, 4, D])
    hn2T = P.sb("hn2T", [128, 8, 512], BF16)
    xt3 = P.sb("xt3", [128, D]); pt3 = P.sb("pt3", [128, D])
    tA = P.sb("tA", [128, D]); hn = P.sb("hn3", [128, D]); hnb = P.sb("hnb3", [128, D], BF16)
    rl = P.sb("rl", [128, 512])
    ssq = P.sb("ssq3", [128, 1]); rstd = P.sb("rstd3", [128, 1])
    ot_ = P.sb("ot_", [128, D]); junk = ot_
    accS = P.ps("accS", [128, D]); accH = P.ps("accH", [128, D])
    pst = P.ps("pst3", [128, D], BF16)
    psa = [P.ps("psa%d" % i, [128, 512]) for i in range(2)]
    P.dma("pool", lambda e: e.dma_start(out=wob[:], in_=w_out.rearrange("(kc p) n -> p kc n", p=128)), "D_wob", writes=["wobraw"])
    P.dma("sp", lambda e: e.dma_start(out=mg[:], in_=mixg), "D_mg", writes=["mg"])
    for kc in range(8):
        P.op("pool", lambda e, kc=kc: e.tensor_scalar(out=wob[:, kc, :], in0=wob[:, kc, :], scalar1=mg[:, kc:kc + 1], scalar2=None, op0=ALU.mult), reads=["wobraw", "mg"], writes=["wob"])
    ymix_v = ymix_d.rearrange("(kc p) t -> p kc t", p=128)
    w1b_v = w1b_d.rearrange("(kc p) n -> p kc n", p=128)
    w2b_v = w2b_d.rearrange("(ft p) n -> p ft n", p=128)
    wk = 0
    for tb in range(8):
        c0 = tb * 512
        P.dma("sp", lambda e, c0=c0: e.dma_start(out=ymb[:], in_=ymix_v[:, :, c0:c0 + 512]), "D_ymb", reads=["ymix_d"], writes=["ymb"])
        for sub in range(4):
            tt = tb * 4 + sub
            for (acc_, k0) in ((accS, 0), (accH, 4)):
                for half in range(2):
                    for kc in range(4):
                        P.op("pe", lambda e, acc_=acc_, k0=k0, half=half, kc=kc, sub=sub: e.matmul(acc_[:, half * 512:(half + 1) * 512], lhsT=ymb[:, k0 + kc, sub * 128:(sub + 1) * 128], rhs=wob[:, k0 + kc, half * 512:(half + 1) * 512], start=(kc == 0), stop=(kc == 3)),
                             reads=["ymb", "wob"], writes=[acc_.name])
            P.dma("sp", lambda e, tt=tt: e.dma_start(out=xt3[:], in_=x[tt * 128:(tt + 1) * 128, :]), "D_xt3", writes=["xt3"])
            P.dma("sp", lambda e, tt=tt: e.dma_start(out=pt3[:], in_=pos[tt * 128:(tt + 1) * 128, :]), "D_pt3", writes=["pt3"])
            P.op("pool", lambda e: e.tensor_tensor(out=xt3[:], in0=xt3[:], in1=pt3[:], op=ALU.add), reads=["xt3", "pt3"], writes=["xt3"])
            P.op("dve", lambda e, tt=tt: e.tensor_scalar(out=tA[:], in0=accS[:], scalar1=rs[:, tt:tt + 1], scalar2=None, op0=ALU.mult), reads=["accS", "rs"], writes=["tA"])
            P.op("dve", lambda e, tt=tt: e.scalar_tensor_tensor(out=tA[:], in0=accH[:], scalar=rh[:, tt:tt + 1], in1=tA[:], op0=ALU.mult, op1=ALU.add), reads=["accH", "rh", "tA"], writes=["tA"])
            P.op("pool", lambda e: e.tensor_tensor(out=tA[:], in0=tA[:], in1=G1[:], op=ALU.mult), reads=["tA", "G1"], writes=["tA"])
            P.op("pool", lambda e, sub=sub: e.tensor_tensor(out=h1b[:, sub, :], in0=tA[:], in1=xt3[:], op=ALU.add), reads=["tA", "xt3"], writes=["h1b%d" % sub])
            h1s = h1b[:, sub, :]
            P.op("act", lambda e, h1s=h1s: e.activation(out=junk[:], in_=h1s, func=AF.Square, accum_out=ssq[:]), reads=["h1b%d" % sub], writes=["ot_", "ssq3"])
            P.op("act", lambda e: e.activation(out=rstd[:], in_=ssq[:], func=AF.Sqrt, scale=1.0 / D, bias=EPS), reads=["ssq3"], writes=["rstd3"])
            P.op("dve", lambda e: e.reciprocal(out=rstd[:], in_=rstd[:]), reads=["rstd3"], writes=["rstd3"])
            P.op("dve", lambda e, h1s=h1s: e.scalar_tensor_tensor(out=hn[:], in0=h1s, scalar=rstd[:, 0:1], in1=A2[:], op0=ALU.mult, op1=ALU.mult), reads=["h1b%d" % sub, "rstd3", "A2"], writes=["hn3"])
            P.op("pool", lambda e: e.tensor_tensor(out=hnb[:], in0=hn[:], in1=B2[:], op=ALU.add), reads=["hn3", "B2"], writes=["hnb3"])
            for kc in range(8):
                P.op("pe", lambda e, kc=kc: e.transpose(out=pst[:, kc * 128:(kc + 1) * 128], in_=hnb[:, kc * 128:(kc + 1) * 128], identity=ident[:]), reads=["hnb3", "ident"], writes=["pst3"])
            P.op("act", lambda e, sub=sub: e.copy(out=hn2T[:, :, sub * 128:(sub + 1) * 128], in_=pst[:].rearrange("p (k t) -> p k t", k=8)), reads=["pst3"], writes=["hn2T"])
        for fg_ in range(4):
            ws = wst1[wk % 2]; wk += 1
            P.dma("sp", lambda e, ws=ws, fg_=fg_: e.dma_start(out=ws[:], in_=w1b_v[:, :, fg_ * 1024:(fg_ + 1) * 1024]), "D_" + ws.name, reads=["w1b_d"], writes=[ws.name])
            for f8 in range(8):
                ft = fg_ * 8 + f8
                pa = psa[ft % 2]
                for kc in range(8):
                    P.op("pe", lambda e, pa=pa, ws=ws, f8=f8, kc=kc: e.matmul(pa[:], lhsT=ws[:, kc, f8 * 128:(f8 + 1) * 128], rhs=hn2T[:, kc, :], start=(kc == 0), stop=(kc == 7)),
                         reads=[ws.name, "hn2T"], writes=[pa.name])
                P.op("dve", lambda e, pa=pa: e.tensor_scalar(out=rl[:], in0=pa[:], scalar1=0.0, scalar2=None, op0=ALU.max), reads=[pa.name], writes=["rl"])
                P.op("act", lambda e, ft=ft: e.activation(out=hid[:, ft, :], in_=rl[:], func=AF.Square), reads=["rl"], writes=["hid"])
        for sp_ in range(2):
            subs = (2 * sp_, 2 * sp_ + 1)
            accs = (accS, accH)
            for fg_ in range(4):
                ws = wst2[wk % 2]; wk += 1
                P.dma("sp", lambda e, ws=ws, fg_=fg_: e.dma_start(out=ws[:], in_=w2b_v[:, fg_ * 8:(fg_ + 1) * 8, :]), "D_" + ws.name, reads=["w2b_d"], writes=[ws.name])
                for si, sub in enumerate(subs):
                    for half in range(2):
                        for f8 in range(8):
                            ft = fg_ * 8 + f8
                            P.op("pe", lambda e, si=si, sub=sub, half=half, f8=f8, ft=ft, ws=ws: e.matmul(accs[si][:, half * 512:(half + 1) * 512], lhsT=hid[:, ft, sub * 128:(sub + 1) * 128], rhs=ws[:, f8, half * 512:(half + 1) * 512], start=(ft == 0), stop=(ft == 31)),
                                 reads=["hid", ws.name], writes=[accs[si].name])
            for si, sub in enumerate(subs):
                tt = tb * 4 + sub
                a_ = accs[si]
                P.op("dve", lambda e, a_=a_: e.tensor_tensor(out=tA[:], in0=a_[:], in1=G2[:], op=ALU.mult), reads=[a_.name, "G2"], writes=["tA"])
                P.op("pool", lambda e, sub=sub: e.tensor_tensor(out=tA[:], in0=tA[:], in1=h1b[:, sub, :], op=ALU.add), reads=["tA", "h1b%d" % sub], writes=["tA"])
                P.op("act", lambda e: e.activation(out=junk[:], in_=tA[:], func=AF.Square, accum_out=ssq[:]), reads=["tA"], writes=["ot_", "ssq3"])
                P.op("act", lambda e: e.activation(out=rstd[:], in_=ssq[:], func=AF.Sqrt, scale=1.0 / D, bias=EPS), reads=["ssq3"], writes=["rstd3"])
                P.op("dve", lambda e: e.reciprocal(out=rstd[:], in_=rstd[:]), reads=["rstd3"], writes=["rstd3"])
                P.op("dve", lambda e: e.scalar_tensor_tensor(out=ot_[:], in0=tA[:], scalar=rstd[:, 0:1], in1=FG[:], op0=ALU.mult, op1=ALU.mult), reads=["tA", "rstd3", "FG"], writes=["ot_"])
                P.dma("act", lambda e, tt=tt: e.dma_start(out=out[tt * 128:(tt + 1) * 128, :], in_=ot_[:]), "DS_ot", reads=["ot_"], writes=["out"])
    P.pop()
    P.barrier()
    P.emit()
    return nc


def _consts():
    n = L
    rows = n // 64
    row = np.repeat(np.arange(rows, dtype=np.float32), 64)
    col = np.tile(np.arange(64, dtype=np.float32), rows)
    quarter = D // 4
    omega = (1.0 / (np.float32(10000.0) ** (np.arange(quarter, dtype=np.float32) / np.float32(quarter)))).astype(np.float32)

    def enc(p):
        ang = (p[:, None] * omega[None, :]).astype(np.float32)
        return np.concatenate([np.sin(ang), np.cos(ang)], axis=-1)
    pos = np.concatenate([enc(row), enc(col)], axis=-1).astype(np.float32)
    t = np.linspace(0.0, 1.0, n, dtype=np.float32)[:, None]
    w = (np.float32(2.0 * math.pi) * np.arange(n, dtype=np.float32) / np.float32(n)).astype(np.float32)
    bands = np.linspace(1e-4, 15, 16, dtype=np.float32)
    ang = (w[:, None] * bands[None, :]).astype(np.float32)
    emb = np.concatenate([t, np.cos(ang), -np.sin(ang)], axis=-1).astype(np.float32)
    embT = np.ascontiguousarray(emb.T)
    embTr = embT.copy()
    embTr[:, 1:] = embT[:, :0:-1]
    embT2 = np.ascontiguousarray(np.stack([embT, embTr], 0))
    tt_ = t[:, 0]
    tau = np.zeros((64, 2, 64), np.float32)
    tau[:, 0, :] = tt_.reshape(64, 64)
    tr = np.empty(n, np.float32); tr[0] = 1.0; tr[1:] = tt_[:0:-1]
    tau[:, 1, :] = tr.reshape(64, 64)
    return pos, embT2, tau


def _hy_tables():
    import ml_dtypes
    N = 8192
    n1 = np.arange(64)[:, None]; k1 = np.arange(128)[None, :]
    ph = 2 * np.pi * (k1 + 0.5) * n1 / 128.0
    F1lo = np.concatenate([np.cos(ph), -np.sin(ph)], 1)
    ph2 = 2 * np.pi * (k1 + 0.5) * (n1 + 64) / 128.0
    F1hi = np.concatenate([np.cos(ph2), -np.sin(ph2)], 1)
    F1 = np.stack([F1lo, F1hi], 1).reshape(64, 512)
    n2 = np.arange(64)[:, None, None]; kk1 = np.arange(128)[None, :, None]; k2 = np.arange(32)[None, None, :]
    th = 2 * np.pi * n2 * (kk1 + 0.5 + 128 * k2) / 8192.0
    Gr = np.cos(th); Gi = -np.sin(th)
    SA = np.concatenate([Gr, Gi, Gi, Gr], 2)
    SB = np.concatenate([-Gi, Gr, Gr, -Gi], 2)
    G = np.stack([SA, SB], 2)
    G = np.concatenate([G, G], 0).reshape(128, 128 * 2 * 128)
    thT = np.transpose(th, (2, 1, 0))
    gr = np.cos(thT); gi = np.sin(thT)
    col_re = np.concatenate([gr, -gr, -gi, -gi], 0)
    col_im = np.concatenate([gi, -gi, gr, gr], 0)
    GP = np.concatenate([col_re, col_im], 2).reshape(128, 128 * 128)
    k1c = np.arange(128)[:, None]; n1r = np.arange(64)[None, :]
    ph3 = 2 * np.pi * (k1c + 0.5) * n1r / 128.0
    E = (np.stack([np.cos(ph3), -np.sin(ph3)], 1) * (2.0 / N)).reshape(128, 128)
    bf = lambda a: np.ascontiguousarray(a.astype(np.float32).astype(ml_dtypes.bfloat16))
    return bf(F1), bf(G), bf(GP), bf(E)


def kernel(**inp):
    f = lambda a: np.ascontiguousarray(np.asarray(a, dtype=np.float32))
    pos, embT2, tauX = _consts()
    F1t, Gt, GPt, Et = _hy_tables()
    B = 8
    x = f(inp["x"]); c = f(inp["c"]); ctx = f(inp["ctx"]); c_ctx = f(inp["c_ctx"])
    a_re = f(inp["s5_a_re"])[0]; a_im = f(inp["s5_a_im"])[0]; ls = f(inp["s5_log_step"])[0]
    b_re = f(inp["s5_b_re"])[0]; b_im = f(inp["s5_b_im"])[0]; c_re = f(inp["s5_c_re"])[0]; c_im = f(inp["s5_c_im"])[0]
    s5p = np.zeros((2, 3, 128, 16), np.float32)
    bpad = np.zeros((2, 128, 32, 2, 128), np.float32)
    clay = np.zeros((2, 2, 128, 16, 16), np.float32)
    for d_ in range(2):
        for g in range(32):
            gh, g16 = g // 16, g % 16
            s5p[d_, 0, gh * 64:(gh + 1) * 64, g16] = a_re[d_, g]
            s5p[d_, 1, gh * 64:(gh + 1) * 64, g16] = a_im[d_, g]
            s5p[d_, 2, gh * 64:(gh + 1) * 64, g16] = ls[d_, g]
            r0 = (g % 8) * 16
            bpad[d_, r0:r0 + 16, g, 0, gh * 64:(gh + 1) * 64] = b_re[d_, g].T
            bpad[d_, r0:r0 + 16, g, 1, gh * 64:(gh + 1) * 64] = b_im[d_, g].T
            clay[d_, 0, gh * 64:(gh + 1) * 64, g16, :] = c_re[d_, g].T
            clay[d_, 1, gh * 64:(gh + 1) * 64, g16, :] = c_im[d_, g].T
    bpad = bpad.reshape(2, 128, 32 * 2 * 128)
    clay = clay.reshape(2, 2, 128, 256)
    fm = lambda v, nt: np.ascontiguousarray(np.asarray(v, np.float32).reshape(nt, 128).T)
    cwv = f(inp["hy_conv_w"])[0]
    cw = np.ascontiguousarray(cwv.reshape(3, 12, 128).transpose(2, 1, 0))
    cb = fm(f(inp["hy_conv_b"])[0], 12)
    dec = f(inp["hy_decay"])[0]; hbv = f(inp["hy_bias"])[0]
    hdec_row = np.ascontiguousarray(dec.reshape(1, 1024))
    hbias_row = np.ascontiguousarray(hbv.reshape(1, 1024))
    fvec = np.ascontiguousarray(np.stack([f(inp["hy_f_b1"])[0], f(inp["hy_f_b2"])[0], f(inp["hy_f_freq"])[0]], axis=1))
    mixg = fm(np.concatenate([f(inp["mix_g_s5"])[0], f(inp["mix_g_hy"])[0]]), 8)
    shared = {
        "pos": pos, "embT": embT2, "tauX": tauX, "Gtab": Gt, "GPtab": GPt, "F1tab": F1t, "Etab": Et,
        "hdec_row": hdec_row, "hbias_row": hbias_row,
        "ada_w": f(inp["ada_w"])[0], "ada_b": f(inp["ada_b"]),
        "g1": f(inp["norm1_g"]), "g2": f(inp["norm2_g"]), "fg": f(inp["final_g"])[None, :],
        "w_in": f(inp["w_in"])[0], "s5p": s5p, "bpad": bpad, "clay": clay,
        "s5d": fm(f(inp["s5_d"])[0], 4), "glu_w": f(inp["s5_glu_w"])[0], "glu_b": fm(f(inp["s5_glu_b"])[0], 8),
        "cw": cw, "cb": cb, "fw1": f(inp["hy_f_w1"])[0], "fw2": f(inp["hy_f_w2"])[0], "fw3": f(inp["hy_f_w3"])[0],
        "fvec": fvec, "mixg": mixg,
        "w_out": f(inp["w_out"])[0], "w1": f(inp["mlp_w1"])[0], "w2": f(inp["mlp_w2"])[0],
    }
    in_maps = []
    for b in range(B):
        cc = np.stack([c[b], c_ctx], axis=1).reshape(8, 128, 2).transpose(1, 0, 2)
        m = dict(shared)
        m["x"] = x[b]; m["ctx"] = ctx[b]; m["cc"] = np.ascontiguousarray(cc)
        in_maps.append(m)
    nc = build_program()
    res = run_bass_kernel_spmd(nc, in_maps, core_ids=list(range(B)))
    return np.stack([np.asarray(r["out"], dtype=np.float32) for r in res.results], axis=0)
```

```python
import math
import os
import numpy as np
import concourse.bass as bass
import concourse.mybir as mybir
from concourse.bass_utils import run_bass_kernel_spmd
from contextlib import ExitStack

F32 = mybir.dt.float32
BF16 = mybir.dt.bfloat16
AF = mybir.ActivationFunctionType
ALU = mybir.AluOpType

ENGS = ("pe", "act", "dve", "pool", "sp")
D = 1024
L = 4096
LC = 256
LT = L + LC
EPS = 1e-6


class Prog:
    def __init__(self, nc):
        self.nc = nc
        self.base = ExitStack()
        self.stacks = [self.base]
        self.ops = {e: [] for e in ENGS}
        self.cnt = {e: 0 for e in ENGS}
        self.sems = {}
        self.dcnt = {}
        self.last_w = {}
        self.readers = {}
        self.waited = {e: {} for e in ENGS}
        for e in ENGS:
            self.sems["E_" + e] = self.base.enter_context(nc.semaphore("E_" + e))

    def sb(self, name, shape, dt=F32):
        return self.stacks[-1].enter_context(self.nc.sbuf_tensor(name, list(shape), dt))

    def ps(self, name, shape, dt=F32):
        return self.stacks[-1].enter_context(self.nc.psum_tensor(name, list(shape), dt))

    def push(self):
        self.stacks.append(ExitStack())

    def pop(self):
        self.barrier()
        self.stacks.pop().close()

    def _sem(self, name):
        if name not in self.sems:
            self.sems[name] = self.base.enter_context(self.nc.semaphore(name))
            self.dcnt[name] = 0
        return self.sems[name]

    def _deps(self, eng, reads, writes):
        deps = {}

        def need(sv):
            if sv is None:
                return
            s, v = sv
            if deps.get(s, 0) < v:
                deps[s] = v
        for t in reads:
            need(self.last_w.get(t))
        for t in writes:
            need(self.last_w.get(t))
            for r in self.readers.get(t, ()):
                need(r)
        out = []
        for s, v in deps.items():
            if eng == "pe" and s == "E_pe":
                continue
            if eng in ("dve", "act") and s == "E_" + eng and v <= self.cnt[eng] - 1:
                continue
            if self.waited[eng].get(s, 0) >= v:
                continue
            self.waited[eng][s] = v
            out.append((s, v))
        return out

    def _mark(self, reads, writes, sv):
        for t in reads:
            self.readers.setdefault(t, []).append(sv)
        for t in writes:
            self.last_w[t] = sv
            self.readers[t] = []

    def op(self, eng, fn, reads=(), writes=()):
        waits = self._deps(eng, reads, writes)
        self.cnt[eng] += 1
        sv = ("E_" + eng, self.cnt[eng])
        self.ops[eng].append((fn, waits, ("E_" + eng, 1)))
        self._mark(reads, writes, sv)

    def dma(self, q, fn, dsem, reads=(), writes=()):
        self._sem(dsem)
        waits = self._deps(q, reads, writes)
        self.dcnt[dsem] += 16
        sv = (dsem, self.dcnt[dsem])
        self.ops[q].append((fn, waits, (dsem, 16)))
        self._mark(reads, writes, sv)

    def barrier(self):
        for e in ENGS:
            waits = []
            for f in ENGS:
                s = "E_" + f
                if f != e and self.cnt[f] > self.waited[e].get(s, 0):
                    waits.append((s, self.cnt[f]))
                    self.waited[e][s] = self.cnt[f]
            for s, v in self.dcnt.items():
                if v > self.waited[e].get(s, 0):
                    waits.append((s, v))
                    self.waited[e][s] = v
            self.ops[e].append((None, waits, None))

    def emit(self):
        nc = self.nc
        sems = self.sems
        ops = self.ops
        with nc.Block() as block:
            def runner(name):
                def run(e):
                    for fn, waits, inc in ops[name]:
                        for s, v in waits:
                            e.wait_ge(sems[s], v)
                        if fn is not None:
                            fn(e).then_inc(sems[inc[0]], inc[1])
                return run
            block.tensor(runner("pe"))
            block.scalar(runner("act"))
            block.vector(runner("dve"))
            block.gpsimd(runner("pool"))
            block.sync(runner("sp"))
        while self.stacks:
            self.stacks.pop().close()


def build_program(debug=False):
    STOP = int(os.environ.get('HY_STOP', '99'))
    nc = bass.Bass("TRN2", target_bir_lowering=False)
    P = Prog(nc)

    def din(name, shape, dt=F32):
        return nc.dram_tensor(name, list(shape), dt, kind="ExternalInput").ap()

    def dscr(name, shape, dt):
        return nc.dram_tensor(name, list(shape), dt, kind=("ExternalOutput" if debug else "Internal")).ap()

    x = din("x", [L, D]); ctx = din("ctx", [LC, D]); pos = din("pos", [L, D])
    cc = din("cc", [128, 8, 2])
    ada_w = din("ada_w", [D, 6 * D]); ada_b = din("ada_b", [1, 6 * D])
    g1 = din("g1", [1, D]); g2 = din("g2", [1, D]); fg = din("fg", [1, D])
    w_in = din("w_in", [D, 2048])
    s5p = din("s5p", [2, 3, 128, 16])
    bpad = din("bpad", [2, 128, 32 * 2 * 128])
    clay = din("clay", [2, 2, 128, 256])
    s5d = din("s5d", [128, 4])
    glu_w = din("glu_w", [512, 1024]); glu_b = din("glu_b", [128, 8])
    cw = din("cw", [128, 12, 3]); cb = din("cb", [128, 12])
    embT = din("embT", [2, 33, L]); tauX = din("tauX", [64, 2, 64])
    hdec_row = din("hdec_row", [1, 1024]); hbias_row = din("hbias_row", [1, 1024])
    Gtab = din("Gtab", [128, 128 * 2 * 128], BF16); GPtab = din("GPtab", [128, 128 * 128], BF16)
    F1tab = din("F1tab", [64, 512], BF16); Etab = din("Etab", [128, 128], BF16)
    fw1 = din("fw1", [33, 64]); fw2 = din("fw2", [64, 64]); fw3 = din("fw3", [64, 2048])
    fvec = din("fvec", [64, 3])
    mixg = din("mixg", [128, 8])
    w_out = din("w_out", [D, D]); w1 = din("w1", [D, 4 * D]); w2 = din("w2", [4 * D, D])
    out = nc.dram_tensor("out", [L, D], F32, kind="ExternalOutput").ap()

    hnT_d = dscr("hnT_d", [D, LT], BF16)
    z_d = dscr("z_d", [1536, L], F32)
    ymix_d = dscr("ymix_d", [D, L], BF16)
    w1b_d = dscr("w1b_d", [D, 4 * D], BF16)
    w2b_d = dscr("w2b_d", [4 * D, D], BF16)

    ident = P.sb("ident", [128, 128], BF16)
    ones = P.sb("ones", [128, 1], BF16)
    G1 = P.sb("G1", [128, D]); A2 = P.sb("A2", [128, D]); B2 = P.sb("B2", [128, D])
    G2 = P.sb("G2", [128, D]); FG = P.sb("FG", [128, D])
    rs = P.sb("rs", [128, 32]); rh = P.sb("rh", [128, 32])
    ssh = P.sb("ssh", [128, 32])
    P.op("pool", lambda e: e.memset(ident[:], 1.0), writes=["ident"])
    P.op("pool", lambda e: e.affine_select(out=ident[:], in_=ident[:], pattern=[[-1, 128]],
                                          compare_op=ALU.is_equal, fill=0.0, base=0, channel_multiplier=1),
         reads=["ident"], writes=["ident"])
    P.op("pool", lambda e: e.memset(ones[:], 1.0), writes=["ones"])
    P.op("pool", lambda e: e.memset(ssh[:], 0.0), writes=["ssh"])

    for i in range(4):
        P.dma("pool", lambda e, i=i: e.dma_start(out=w1b_d[i * 256:(i + 1) * 256, :], in_=w1[i * 256:(i + 1) * 256, :]),
              "D_w1c", writes=["w1b_d"] if i == 3 else ["w1b_d%d" % i])
        P.dma("pool", lambda e, i=i: e.dma_start(out=w2b_d[i * 1024:(i + 1) * 1024, :], in_=w2[i * 1024:(i + 1) * 1024, :]),
              "D_w2c", writes=["w2b_d"] if i == 3 else ["w2b_d%d" % i])

    P.push()
    A1 = P.sb("A1", [128, D]); B1 = P.sb("B1", [128, D]); A1c = P.sb("A1c", [128, D]); B1c = P.sb("B1c", [128, D])
    P.push()
    ccs = P.sb("ccs", [128, 8, 2]); sc = P.sb("sc", [128, 8, 2])
    screp = [P.sb("screp%d" % w, [128, 8, 128]) for w in range(2)]
    adab = P.sb("adab", [128, 6 * D])
    modc = P.sb("modc", [128, 6 * D]); modx = P.sb("modx", [128, 2 * D])
    grep = P.sb("grep", [128, D])
    wblk = [P.sb("wblk%d" % i, [128, 8, 512]) for i in range(2)]
    psm = [P.ps("psm%d" % i, [128, 512]) for i in range(2)]
    P.dma("sp", lambda e: e.dma_start(out=ccs[:], in_=cc), "D_ccs", writes=["ccs"])
    P.dma("sp", lambda e: e.dma_start(out=adab[:], in_=ada_b.to_broadcast([128, 6 * D])), "D_adab", writes=["adab"])
    P.op("act", lambda e: e.activation(out=sc[:], in_=ccs[:], func=AF.Silu), reads=["ccs"], writes=["sc"])
    for w in range(2):
        for kc in range(8):
            P.op("dve", lambda e, w=w, kc=kc: e.tensor_copy(out=screp[w][:, kc, :], in_=sc[:, kc, w:w + 1].to_broadcast([128, 128])),
                 reads=["sc"], writes=["screp%d" % w])
    adaw_v = ada_w.rearrange("(kc p) n -> p kc n", p=128)
    k = 0
    for blk in range(12):
        wb = wblk[blk % 2]
        P.dma("sp", lambda e, wb=wb, blk=blk: e.dma_start(out=wb[:], in_=adaw_v[:, :, blk * 512:(blk + 1) * 512]),
              "D_" + wb.name, writes=[wb.name])
        for w in range(2 if blk < 4 else 1):
            pm = psm[k % 2]; k += 1
            for kc in range(8):
                P.op("pe", lambda e, pm=pm, w=w, kc=kc, wb=wb: e.matmul(pm[:], lhsT=screp[w][:, kc, :], rhs=wb[:, kc, :], start=(kc == 0), stop=(kc == 7)),
                     reads=["screp%d" % w, wb.name], writes=[pm.name])
            dst = modc if w == 0 else modx
            P.op("dve", lambda e, pm=pm, dst=dst, blk=blk: e.tensor_tensor(out=dst[:, blk * 512:(blk + 1) * 512], in0=pm[:], in1=adab[:, blk * 512:(blk + 1) * 512], op=ALU.add),
                 reads=[pm.name, "adab"], writes=[dst.name + str(blk)])
    allc = ["modc%d" % b for b in range(12)]
    allx = ["modx%d" % b for b in range(4)]

    def rep_load(row, tag):
        P.dma("sp", lambda e: e.dma_start(out=grep[:], in_=row.to_broadcast([128, D])), "D_grep", writes=["grep"])
    rep_load(g1, "g1")
    P.op("dve", lambda e: e.scalar_tensor_tensor(out=A1[:], in0=modc[:, D:2 * D], scalar=1.0, in1=grep[:], op0=ALU.add, op1=ALU.mult), reads=allc + ["grep"], writes=["A1"])
    P.op("dve", lambda e: e.scalar_tensor_tensor(out=A1c[:], in0=modx[:, D:2 * D], scalar=1.0, in1=grep[:], op0=ALU.add, op1=ALU.mult), reads=allx + ["grep"], writes=["A1c"])
    P.op("pool", lambda e: e.tensor_copy(out=B1[:], in_=modc[:, 0:D]), reads=allc, writes=["B1"])
    P.op("pool", lambda e: e.tensor_copy(out=B1c[:], in_=modx[:, 0:D]), reads=allx, writes=["B1c"])
    P.op("pool", lambda e: e.tensor_copy(out=G1[:], in_=modc[:, 2 * D:3 * D]), reads=allc, writes=["G1"])
    P.op("pool", lambda e: e.tensor_copy(out=B2[:], in_=modc[:, 3 * D:4 * D]), reads=allc, writes=["B2"])
    P.op("pool", lambda e: e.tensor_copy(out=G2[:], in_=modc[:, 5 * D:6 * D]), reads=allc, writes=["G2"])
    rep_load(g2, "g2")
    P.op("dve", lambda e: e.scalar_tensor_tensor(out=A2[:], in0=modc[:, 4 * D:5 * D], scalar=1.0, in1=grep[:], op0=ALU.add, op1=ALU.mult), reads=allc + ["grep"], writes=["A2"])
    P.dma("sp", lambda e: e.dma_start(out=FG[:], in_=fg.to_broadcast([128, D])), "D_FG", writes=["FG"])
    P.pop()

    P.push()
    xt = [P.sb("xt%d" % i, [128, D]) for i in range(2)]
    pt = [P.sb("pt%d" % i, [128, D]) for i in range(2)]
    junks = [P.sb("junk%d" % i, [128, D]) for i in range(2)]
    hns = [P.sb("hn%d" % i, [128, D]) for i in range(2)]; hnbs = [P.sb("hnb%d" % i, [128, D], BF16) for i in range(2)]
    ssqs = [P.sb("ssq%d" % i, [128, 1]) for i in range(2)]; rstds = [P.sb("rstd%d" % i, [128, 1]) for i in range(2)]
    hT = [P.sb("hT%d" % i, [128, 8, 128], BF16) for i in range(2)]
    psts = [P.ps("pst%d" % i, [128, D], BF16) for i in range(2)]
    hnT_v = hnT_d.rearrange("(kc p) t -> p kc t", p=128)

    def norm_mod_T(src, Arep, Brep, dstT, par):
        junk = junks[par]; hn = hns[par]; hnb = hnbs[par]; ssq = ssqs[par]; rstd = rstds[par]; pst = psts[par]
        P.op("act", lambda e: e.activation(out=junk[:], in_=src[:], func=AF.Square, accum_out=ssq[:]), reads=[src.name], writes=[junk.name, ssq.name])
        P.op("act", lambda e: e.activation(out=rstd[:], in_=ssq[:], func=AF.Sqrt, scale=1.0 / D, bias=EPS), reads=[ssq.name], writes=[rstd.name])
        P.op("dve", lambda e: e.reciprocal(out=rstd[:], in_=rstd[:]), reads=[rstd.name], writes=[rstd.name])
        P.op("dve", lambda e: e.scalar_tensor_tensor(out=hn[:], in0=src[:], scalar=rstd[:, 0:1], in1=Arep[:], op0=ALU.mult, op1=ALU.mult), reads=[src.name, rstd.name, Arep.name], writes=[hn.name])
        P.op("pool", lambda e: e.tensor_tensor(out=hnb[:], in0=hn[:], in1=Brep[:], op=ALU.add), reads=[hn.name, Brep.name], writes=[hnb.name])
        for kc in range(8):
            P.op("pe", lambda e, kc=kc: e.transpose(out=pst[:, kc * 128:(kc + 1) * 128], in_=hnb[:, kc * 128:(kc + 1) * 128], identity=ident[:]), reads=[hnb.name, "ident"], writes=[pst.name])
        P.op("act", lambda e: e.copy(out=dstT[:].rearrange("p k t -> p (k t)"), in_=pst[:]), reads=[pst.name], writes=[dstT.name])

    for tt in range(34):
        xb = xt[tt % 2]; pb = pt[tt % 2]; hb = hT[tt % 2]
        if tt < 32:
            P.dma("sp", lambda e, xb=xb, tt=tt: e.dma_start(out=xb[:], in_=x[tt * 128:(tt + 1) * 128, :]), "D_" + xb.name, writes=[xb.name])
            P.dma("sp", lambda e, pb=pb, tt=tt: e.dma_start(out=pb[:], in_=pos[tt * 128:(tt + 1) * 128, :]), "D_" + pb.name, writes=[pb.name])
            P.op("pool", lambda e, xb=xb, pb=pb: e.tensor_tensor(out=xb[:], in0=xb[:], in1=pb[:], op=ALU.add), reads=[xb.name, pb.name], writes=[xb.name])
            norm_mod_T(xb, A1, B1, hb, tt % 2)
        else:
            c0 = (tt - 32) * 128
            P.dma("sp", lambda e, xb=xb, c0=c0: e.dma_start(out=xb[:], in_=ctx[c0:c0 + 128, :]), "D_" + xb.name, writes=[xb.name])
            norm_mod_T(xb, A1c, B1c, hb, tt % 2)
        P.dma("act", lambda e, hb=hb, tt=tt: e.dma_start(out=hnT_v[:, :, tt * 128:(tt + 1) * 128], in_=hb[:]), "DS_" + hb.name, reads=[hb.name], writes=["hnT_d%d" % tt])
    P.pop()
    P.pop()

    if int(os.environ.get('PH_STOP', '99')) < 1:
        P.barrier(); P.emit(); return nc
    P.push()
    uT = P.sb("uT", [128, 4, LT], BF16)
    P.push()
    winb = P.sb("winb", [128, 8, 2048], BF16)
    hblk = [P.sb("hblk%d" % i, [128, 8, 512], BF16) for i in range(2)]
    zb = [P.sb("zb%d" % i, [128, 512]) for i in range(2)]
    psp = [P.ps("psp%d" % i, [128, 512]) for i in range(2)]
    P.dma("pool", lambda e: e.dma_start(out=winb[:], in_=w_in.rearrange("(kc p) n -> p kc n", p=128)), "D_winb", writes=["winb"])
    k = 0
    for tb in range(9):
        hb = hblk[tb % 2]
        nt = 512 if tb < 8 else 256
        c0 = tb * 512
        P.dma("sp", lambda e, hb=hb, c0=c0, nt=nt: e.dma_start(out=hb[:, :, 0:nt], in_=hnT_v[:, :, c0:c0 + nt]), "D_" + hb.name,
              reads=["hnT_d%d" % t for t in range(tb * 4, min(34, tb * 4 + 4))], writes=[hb.name])
        for ft in range(16 if tb < 8 else 4):
            pp = psp[k % 2]; k += 1
            for kc in range(8):
                P.op("pe", lambda e, pp=pp, hb=hb, kc=kc, ft=ft, nt=nt: e.matmul(pp[:, 0:nt], lhsT=winb[:, kc, ft * 128:(ft + 1) * 128], rhs=hb[:, kc, 0:nt], start=(kc == 0), stop=(kc == 7)),
                     reads=["winb", hb.name], writes=[pp.name])
            if ft < 4:
                P.op("act", lambda e, pp=pp, ft=ft, c0=c0, nt=nt: e.copy(out=uT[:, ft, c0:c0 + nt], in_=pp[:, 0:nt]), reads=[pp.name], writes=["uT"])
            else:
                z = zb[ft % 2]
                P.op("dve", lambda e, pp=pp, z=z: e.tensor_copy(out=z[:], in_=pp[:]), reads=[pp.name], writes=[z.name])
                P.dma("act", lambda e, z=z, ft=ft, c0=c0: e.dma_start(out=z_d[(ft - 4) * 128:(ft - 3) * 128, c0:c0 + 512], in_=z[:]), "DS_" + z.name, reads=[z.name], writes=["z_d"])
    P.pop()

    if int(os.environ.get('PH_STOP', '99')) < 2:
        P.barrier(); P.emit(); return nc
    P.push()
    ybuf = P.sb("ybuf", [128, 4, L], BF16)
    P.push()
    bpb = P.sb("bpb", [128, 32, 2, 128], BF16)
    cpad = P.sb("cpad", [128, 32, 2, 128], BF16)
    NB = 64
    braw = [P.sb("braw%d" % i, [128, 16, 2, NB], BF16) for i in range(4)]
    Ck = P.sb("Ck", [128, 16, NB], BF16); Sk = P.sb("Sk", [128, 16, NB], BF16)
    D0 = P.sb("D0", [128, 16, NB])
    Tm1 = P.sb("Tm1", [128, 16, NB], BF16); Tm2 = P.sb("Tm2", [128, 16, NB], BF16)
    Tm3 = P.sb("Tm3", [128, 16, NB], BF16); Tm4 = P.sb("Tm4", [128, 16, NB], BF16)
    itm2 = P.sb("itm2", [128, 16])
    prr = P.sb("prr", [128, 16, NB], BF16); pri = P.sb("pri", [128, 16, NB], BF16)
    rrs = [P.sb("rr%d" % i, [128, 16, NB], BF16) for i in range(2)]; rims = [P.sb("rim%d" % i, [128, 16, NB], BF16) for i in range(2)]
    Tp1 = P.sb("Tp1", [128, 16, NB], BF16); Tp2 = P.sb("Tp2", [128, 16, NB], BF16)
    Tp3 = P.sb("Tp3", [128, 16, NB], BF16); Tp4 = P.sb("Tp4", [128, 16, NB], BF16)
    st = [P.sb("st%d" % i, [128, 16, 2, 2 * NB], BF16) for i in range(2)]
    cNr = P.sb("cNr", [128, 16]); cNi = P.sb("cNi", [128, 16]); inj = P.sb("inj", [128, 2, 16]); itmp = P.sb("itmp", [128, 16])
    wr = P.sb("wr", [128, 16]); wi = P.sb("wi", [128, 16])
    Aco = P.sb("Aco", [128, 2, 16, 2])
    d5 = P.sb("d5", [128, 4])
    pr = P.sb("pr", [128, 3, 16])
    cl = P.sb("cl", [128, 2, 16, 16])
    tsm = [P.sb("tsm%d" % i, [128, 16]) for i in range(12)]
    cpr = P.sb("cpr", [128, 16, 16]); cpi = P.sb("cpi", [128, 16, 16]); ctmp = P.sb("ctmp", [128, 16, 16])
    ytmps = [P.sb("ytmp%d" % i, [128, 128]) for i in range(2)]
    psb = [P.ps("psb%d" % i, [128, 2, 256]) for i in range(2)]
    psy = [P.ps("psy%d" % i, [128, 512]) for i in range(4)]
    P.dma("sp", lambda e: e.dma_start(out=d5[:], in_=s5d), "D_d5", writes=["d5"])

    def small(eng, fn, reads, writes):
        P.op(eng, fn, reads=reads, writes=writes)

    for pas in range(2):
        dr = 1 - pas
        P.dma("pool", lambda e, dr=dr: e.dma_start(out=bpb[:].rearrange("p g r m -> p (g r m)"), in_=bpad[dr]), "D_bpb", writes=["bpb"])
        P.dma("sp", lambda e, dr=dr: e.dma_start(out=pr[:], in_=s5p[dr].rearrange("k p g -> p k g")), "D_pr", writes=["pr"])
        P.dma("sp", lambda e, dr=dr: e.dma_start(out=cl[:].rearrange("p r g c -> p r (g c)"), in_=clay[dr].rearrange("r p m -> p r m")), "D_cl", writes=["cl"])
        are = pr[:, 0, :]; aim = pr[:, 1, :]; lst = pr[:, 2, :]
        dt_, mag, ang, cs, sn, t1, t2, t3, lr, li, kr, ki = [t[:] for t in tsm]
        T = ["tsm"]
        small("act", lambda e: e.activation(out=dt_, in_=lst, func=AF.Exp), ["pr"], T)
        small("dve", lambda e: e.tensor_tensor(out=mag, in0=are, in1=dt_, op=ALU.mult), ["pr"] + T, T)
        small("act", lambda e: e.activation(out=mag, in_=mag, func=AF.Exp), T, T)
        small("dve", lambda e: e.scalar_tensor_tensor(out=ang, in0=aim, scalar=1.0 / 16, in1=dt_, op0=ALU.mult, op1=ALU.mult), ["pr"] + T, T)
        small("act", lambda e: e.activation(out=sn, in_=ang, func=AF.Sin), T, T)
        small("act", lambda e: e.activation(out=cs, in_=ang, func=AF.Sin, bias=math.pi / 2), T, T)
        for _ in range(4):
            small("dve", lambda e: e.tensor_tensor(out=t1, in0=cs, in1=cs, op=ALU.mult), T, T)
            small("dve", lambda e: e.tensor_tensor(out=t2, in0=sn, in1=sn, op=ALU.mult), T, T)
            small("dve", lambda e: e.tensor_tensor(out=t3, in0=cs, in1=sn, op=ALU.mult), T, T)
            small("dve", lambda e: e.tensor_tensor(out=cs, in0=t1, in1=t2, op=ALU.subtract), T, T)
            small("dve", lambda e: e.tensor_scalar(out=sn, in0=t3, scalar1=2.0, scalar2=None, op0=ALU.mult), T, T)
        small("dve", lambda e: e.tensor_tensor(out=lr, in0=mag, in1=cs, op=ALU.mult), T, T)
        small("dve", lambda e: e.tensor_tensor(out=li, in0=mag, in1=sn, op=ALU.mult), T, T)
        small("dve", lambda e: e.tensor_tensor(out=t1, in0=are, in1=are, op=ALU.mult), ["pr"] + T, T)
        small("dve", lambda e: e.tensor_tensor(out=t2, in0=aim, in1=aim, op=ALU.mult), ["pr"] + T, T)
        small("dve", lambda e: e.tensor_tensor(out=t1, in0=t1, in1=t2, op=ALU.add), T, T)
        small("dve", lambda e: e.reciprocal(out=t1, in_=t1), T, T)
        small("dve", lambda e: e.tensor_scalar(out=t2, in0=lr, scalar1=-1.0, scalar2=None, op0=ALU.add), T, T)
        small("dve", lambda e: e.tensor_tensor(out=kr, in0=t2, in1=are, op=ALU.mult), ["pr"] + T, T)
        small("dve", lambda e: e.tensor_tensor(out=t3, in0=li, in1=aim, op=ALU.mult), ["pr"] + T, T)
        small("dve", lambda e: e.tensor_tensor(out=kr, in0=kr, in1=t3, op=ALU.add), T, T)
        small("dve", lambda e: e.tensor_tensor(out=kr, in0=kr, in1=t1, op=ALU.mult), T, T)
        small("dve", lambda e: e.tensor_tensor(out=ki, in0=li, in1=are, op=ALU.mult), ["pr"] + T, T)
        small("dve", lambda e: e.tensor_tensor(out=t3, in0=t2, in1=aim, op=ALU.mult), ["pr"] + T, T)
        small("dve", lambda e: e.tensor_tensor(out=ki, in0=ki, in1=t3, op=ALU.subtract), T, T)
        small("dve", lambda e: e.tensor_tensor(out=ki, in0=ki, in1=t1, op=ALU.mult), T, T)
        small("dve", lambda e: e.tensor_copy(out=Aco[:, 0, :, 0], in_=lr), T, ["Aco"])
        small("dve", lambda e: e.tensor_copy(out=Aco[:, 0, :, 1], in_=lr), T, ["Aco"])
        small("dve", lambda e: e.tensor_scalar(out=Aco[:, 1, :, 0], in0=li, scalar1=-1.0, scalar2=None, op0=ALU.mult), T, ["Aco"])
        small("dve", lambda e: e.tensor_copy(out=Aco[:, 1, :, 1], in_=li), T, ["Aco"])
        krb = tsm[10][:].unsqueeze(2).to_broadcast([128, 16, 16]); kib = tsm[11][:].unsqueeze(2).to_broadcast([128, 16, 16])
        small("dve", lambda e: e.tensor_tensor(out=cpr[:], in0=cl[:, 0], in1=krb, op=ALU.mult), ["cl"] + T, ["cpr"])
        small("dve", lambda e: e.tensor_tensor(out=ctmp[:], in0=cl[:, 1], in1=kib, op=ALU.mult), ["cl"] + T, ["ctmp"])
        small("dve", lambda e: e.tensor_tensor(out=cpr[:], in0=cpr[:], in1=ctmp[:], op=ALU.subtract), ["cpr", "ctmp"], ["cpr"])
        small("dve", lambda e: e.tensor_tensor(out=cpi[:], in0=cl[:, 0], in1=kib, op=ALU.mult), ["cl"] + T, ["cpi"])
        small("dve", lambda e: e.tensor_tensor(out=ctmp[:], in0=cl[:, 1], in1=krb, op=ALU.mult), ["cl", "cpr"] + T, ["ctmp"])
        small("dve", lambda e: e.tensor_tensor(out=cpi[:], in0=cpi[:], in1=ctmp[:], op=ALU.add), ["cpi", "ctmp"], ["cpi"])
        small("dve", lambda e: e.tensor_scalar(out=cpi[:], in0=cpi[:], scalar1=-1.0, scalar2=None, op0=ALU.mult), ["cpi"], ["cpi"])
        small("pool", lambda e: e.memset(cpad[:], 0.0), [], ["cpad"])
        for gh in range(2):
            for g16 in range(16):
                g = gh * 16 + g16
                for ri, src in ((0, cpr), (1, cpi)):
                    small("pool", lambda e, gh=gh, g16=g16, g=g, ri=ri, src=src: e.tensor_copy(
                        out=cpad[gh * 64:(gh + 1) * 64, g, ri, (g % 8) * 16:(g % 8) * 16 + 16], in_=src[gh * 64:(gh + 1) * 64, g16, :]),
                        ["cpr", "cpi"], ["cpad"])
        small("dve", lambda e: e.tensor_copy(out=wr[:], in_=cs), T, ["wri"])
        small("dve", lambda e: e.tensor_copy(out=wi[:], in_=sn), T, ["wri"])
        rev = (dr == 1)
        i0_ = NB - 1 if rev else 0
        P.push()
        Ckf = P.sb("Ckf%d" % pas, [128, 16, NB]); Skf = P.sb("Skf%d" % pas, [128, 16, NB])
        T1 = P.sb("T1_%d" % pas, [128, 16, NB // 2]); T2 = P.sb("T2_%d" % pas, [128, 16, NB // 2])
        small("pool", lambda e, i0_=i0_: e.memset(Ckf[:, :, i0_:i0_ + 1], 1.0), [], ["tab"])
        small("pool", lambda e, i0_=i0_: e.memset(Skf[:, :, i0_:i0_ + 1], 0.0), [], ["tab"])
        m = 1
        while m < NB:
            wrb = wr[:].unsqueeze(2).to_broadcast([128, 16, m]); wib = wi[:].unsqueeze(2).to_broadcast([128, 16, m])
            src = slice(NB - m, NB) if rev else slice(0, m)
            dst = slice(NB - 2 * m, NB - m) if rev else slice(m, 2 * m)
            small("dve", lambda e, m=m, wrb=wrb, src=src: e.tensor_tensor(out=T1[:, :, 0:m], in0=Ckf[:, :, src], in1=wrb, op=ALU.mult), ["tab", "wri"], ["T1"])
            small("pool", lambda e, m=m, wib=wib, src=src: e.tensor_tensor(out=T2[:, :, 0:m], in0=Skf[:, :, src], in1=wib, op=ALU.mult), ["tab", "wri"], ["T2"])
            small("dve", lambda e, m=m, dst=dst: e.tensor_tensor(out=Ckf[:, :, dst], in0=T1[:, :, 0:m], in1=T2[:, :, 0:m], op=ALU.subtract), ["T1", "T2"], ["tab"])
            small("dve", lambda e, m=m, wib=wib, src=src: e.tensor_tensor(out=T1[:, :, 0:m], in0=Ckf[:, :, src], in1=wib, op=ALU.mult), ["tab", "wri"], ["T1"])
            small("pool", lambda e, m=m, wrb=wrb, src=src: e.tensor_tensor(out=T2[:, :, 0:m], in0=Skf[:, :, src], in1=wrb, op=ALU.mult), ["tab", "wri"], ["T2"])
            small("dve", lambda e, m=m, dst=dst: e.tensor_tensor(out=Skf[:, :, dst], in0=T1[:, :, 0:m], in1=T2[:, :, 0:m], op=ALU.add), ["T1", "T2"], ["tab"])
            small("dve", lambda e: e.tensor_tensor(out=t1, in0=wr[:], in1=wr[:], op=ALU.mult), ["wri"] + T, T)
            small("dve", lambda e: e.tensor_tensor(out=t2, in0=wi[:], in1=wi[:], op=ALU.mult), ["wri"] + T, T)
            small("dve", lambda e: e.tensor_tensor(out=t3, in0=wr[:], in1=wi[:], op=ALU.mult), ["wri"] + T, T)
            small("dve", lambda e: e.tensor_tensor(out=wr[:], in0=t1, in1=t2, op=ALU.subtract), T, ["wri"])
            small("dve", lambda e: e.tensor_scalar(out=wi[:], in0=t3, scalar1=2.0, scalar2=None, op0=ALU.mult), T, ["wri"])
            m *= 2
        small("act", lambda e: e.copy(out=Ck[:], in_=Ckf[:]), ["tab"], ["tabb"])
        small("act", lambda e: e.copy(out=Sk[:], in_=Skf[:]), ["tab"], ["tabb"])
        P.pop()
        small("dve", lambda e: e.tensor_tensor(out=cNr[:], in0=wr[:], in1=mag, op=ALU.mult), ["wri"] + T, ["cN"])
        small("dve", lambda e: e.tensor_tensor(out=cNi[:], in0=wi[:], in1=mag, op=ALU.mult), ["wri"] + T, ["cN"])
        small("dve", lambda e: e.tensor_copy(out=D0[:], in_=mag.unsqueeze(2).to_broadcast([128, 16, NB])), T, ["D0"])
        small("dve", lambda e, i0_=i0_: e.memset(D0[:, :, i0_:i0_ + 1], 0.0), ["D0"], ["D0"])
        small("pool", lambda e: e.memset(inj[:], 0.0), [], ["inj"])
        cblocks = [(L + b * NB, False) for b in range(LC // NB)]
        lblocks = [(b * NB, True) for b in range(L // NB)]
        blocks = (cblocks + lblocks) if dr == 0 else (cblocks[::-1] + lblocks[::-1])
        jf = NB - 1 if rev else 0
        jl = 0 if rev else NB - 1
        fl = (lambda ap: ap.rearrange("p g t -> p (g t)")[:, ::-1]) if rev else (lambda ap: ap.rearrange("p g t -> p (g t)"))
        def pair_base(p_):
            return min(blocks[2 * p_][0], blocks[2 * p_ + 1][0])

        def emit_BU2(p_):
            base = pair_base(p_)
            for g16 in range(16):
                pb_ = psb[g16 % 2]
                for ri in range(2):
                    P.op("pe", lambda e, pb_=pb_, g16=g16, ri=ri: e.matmul(pb_[:, ri, 0:2 * NB], lhsT=bpb[:, g16, ri, :], rhs=uT[:, g16 // 8, base:base + 2 * NB], start=True, stop=False),
                         reads=["bpb", "uT"], writes=[pb_.name])
                    P.op("pe", lambda e, pb_=pb_, g16=g16, ri=ri: e.matmul(pb_[:, ri, 0:2 * NB], lhsT=bpb[:, 16 + g16, ri, :], rhs=uT[:, 2 + g16 // 8, base:base + 2 * NB], start=False, stop=True),
                         reads=["bpb", "uT"], writes=[pb_.name])
                for b_ in (2 * p_, 2 * p_ + 1):
                    br = braw[b_ % 4]; off = blocks[b_][0] - base
                    P.op("act", lambda e, pb_=pb_, g16=g16, br=br, off=off: e.copy(out=br[:, g16, :, :], in_=pb_[:, :, off:off + NB]), reads=[pb_.name], writes=[br.name])

        def emit_scan(bi, jf=jf, jl=jl, fl=fl):
            br = braw[bi % 4]; rr = rrs[bi % 2]; rim = rims[bi % 2]
            bre = br[:, :, 0, :]; bim = br[:, :, 1, :]
            P.op("dve", lambda e: e.tensor_tensor(out=Tm1[:], in0=bre, in1=Ck[:], op=ALU.mult), reads=[br.name, "tabb"], writes=["Tm1"])
            P.op("dve", lambda e: e.tensor_tensor(out=Tm2[:], in0=bim, in1=Sk[:], op=ALU.mult), reads=[br.name, "tabb"], writes=["Tm2"])
            P.op("dve", lambda e: e.tensor_tensor(out=Tm3[:], in0=bre, in1=Sk[:], op=ALU.mult), reads=[br.name, "tabb"], writes=["Tm3"])
            P.op("dve", lambda e: e.tensor_tensor(out=Tm4[:], in0=bim, in1=Ck[:], op=ALU.mult), reads=[br.name, "tabb"], writes=["Tm4"])
            P.op("dve", lambda e: e.tensor_tensor(out=prr[:], in0=Tm1[:], in1=Tm2[:], op=ALU.add), reads=["Tm1", "Tm2"], writes=["prr"])
            P.op("dve", lambda e: e.tensor_tensor(out=pri[:], in0=Tm4[:], in1=Tm3[:], op=ALU.subtract), reads=["Tm3", "Tm4"], writes=["pri"])
            P.op("dve", lambda e: e.tensor_tensor(out=prr[:, :, jf], in0=prr[:, :, jf], in1=inj[:, 0, :], op=ALU.add), reads=["prr", "inj"], writes=["prr"])
            P.op("dve", lambda e: e.tensor_tensor(out=pri[:, :, jf], in0=pri[:, :, jf], in1=inj[:, 1, :], op=ALU.add), reads=["pri", "inj"], writes=["pri"])
            P.op("dve", lambda e: e.tensor_tensor_scan(out=fl(rr[:]), data0=fl(D0[:]), data1=fl(prr[:]), initial=0.0, op0=ALU.mult, op1=ALU.add), reads=["prr", "D0"], writes=[rr.name])
            P.op("dve", lambda e: e.tensor_tensor_scan(out=fl(rim[:]), data0=fl(D0[:]), data1=fl(pri[:]), initial=0.0, op0=ALU.mult, op1=ALU.add), reads=["pri", "D0"], writes=[rim.name])
            P.op("dve", lambda e: e.tensor_tensor(out=inj[:, 0, :], in0=rr[:, :, jl], in1=cNr[:], op=ALU.mult), reads=[rr.name, "cN", "prr", "pri"], writes=["inj"])
            P.op("dve", lambda e: e.tensor_tensor(out=itmp[:], in0=rim[:, :, jl], in1=cNi[:], op=ALU.mult), reads=[rim.name, "cN"], writes=["itmp"])
            P.op("dve", lambda e: e.tensor_tensor(out=itm2[:], in0=rr[:, :, jl], in1=cNi[:], op=ALU.mult), reads=[rr.name, "cN"], writes=["itm2"])
            P.op("dve", lambda e: e.tensor_tensor(out=inj[:, 1, :], in0=rim[:, :, jl], in1=cNr[:], op=ALU.mult), reads=[rim.name, "cN", "inj"], writes=["inj"])
            P.op("dve", lambda e: e.tensor_tensor(out=inj[:, 0, :], in0=inj[:, 0, :], in1=itmp[:], op=ALU.subtract), reads=["inj", "itmp"], writes=["inj"])
            P.op("dve", lambda e: e.tensor_tensor(out=inj[:, 1, :], in0=inj[:, 1, :], in1=itm2[:], op=ALU.add), reads=["inj", "itm2"], writes=["inj"])

        def emit_post_mults(bi):
            rr = rrs[bi % 2]; rim = rims[bi % 2]
            P.op("pool", lambda e: e.tensor_tensor(out=Tp1[:], in0=rr[:], in1=Ck[:], op=ALU.mult), reads=[rr.name, "tabb"], writes=["Tp1"])
            P.op("pool", lambda e: e.tensor_tensor(out=Tp2[:], in0=rim[:], in1=Sk[:], op=ALU.mult), reads=[rim.name, "tabb"], writes=["Tp2"])
            P.op("pool", lambda e: e.tensor_tensor(out=Tp3[:], in0=rr[:], in1=Sk[:], op=ALU.mult), reads=[rr.name, "tabb"], writes=["Tp3"])
            P.op("pool", lambda e: e.tensor_tensor(out=Tp4[:], in0=rim[:], in1=Ck[:], op=ALU.mult), reads=[rim.name, "tabb"], writes=["Tp4"])

        def emit_combine(bi):
            p_ = bi // 2
            sb_ = st[p_ % 2]; off = blocks[bi][0] - pair_base(p_)
            P.op("dve", lambda e: e.tensor_tensor(out=sb_[:, :, 0, off:off + NB], in0=Tp1[:], in1=Tp2[:], op=ALU.subtract), reads=["Tp1", "Tp2"], writes=[sb_.name])
            P.op("dve", lambda e: e.tensor_tensor(out=sb_[:, :, 1, off:off + NB], in0=Tp3[:], in1=Tp4[:], op=ALU.add), reads=["Tp3", "Tp4"], writes=[sb_.name])

        def emit_cproj(p_):
            sb_ = st[p_ % 2]
            for ot in range(4):
                py = psy[ot]; pyn = "psy%d" % ot
                n = 0
                for g in range(ot * 8, ot * 8 + 8):
                    for ri in range(2):
                        P.op("pe", lambda e, py=py, g=g, ri=ri, n=n: e.matmul(py[:, 0:2 * NB], lhsT=cpad[:, g, ri, :], rhs=sb_[:, g % 16, ri, :], start=(n == 0), stop=(n == 15)),
                             reads=["cpad", sb_.name], writes=[pyn])
                        n += 1

        def emit_y(p_, pas=pas):
            c0 = pair_base(p_); W = 2 * NB
            for ot in range(4):
                py = psy[ot]; pyn = "psy%d" % ot
                if pas == 0:
                    P.op("act", lambda e, py=py, ot=ot: e.copy(out=ybuf[:, ot, c0:c0 + W], in_=py[:, 0:W]), reads=[pyn], writes=["ybuf"])
                else:
                    yt_ = ytmps[ot % 2]
                    P.op("dve", lambda e, py=py, ot=ot, yt_=yt_: e.tensor_tensor(out=yt_[:, 0:W], in0=py[:, 0:W], in1=ybuf[:, ot, c0:c0 + W], op=ALU.add), reads=[pyn, "ybuf"], writes=[yt_.name])
                    P.op("dve", lambda e, ot=ot, yt_=yt_: e.scalar_tensor_tensor(out=yt_[:, 0:W], in0=uT[:, ot, c0:c0 + W], scalar=d5[:, ot:ot + 1], in1=yt_[:, 0:W], op0=ALU.mult, op1=ALU.add), reads=["uT", "d5", yt_.name], writes=[yt_.name])
                    P.op("act", lambda e, ot=ot, yt_=yt_: e.activation(out=ybuf[:, ot, c0:c0 + W], in_=yt_[:, 0:W], func=AF.Gelu_apprx_tanh), reads=[yt_.name], writes=["ybuf"])

        nblk = len(blocks)
        emit_BU2(0)
        pend_comb = None
        pend_y = None
        for bi, (c0, islat) in enumerate(blocks):
            if bi % 2 == 0 and bi + 2 < nblk:
                emit_BU2(bi // 2 + 1)
            emit_scan(bi)
            if pend_y is not None:
                emit_y(pend_y)
                pend_y = None
            if pend_comb is not None:
                emit_combine(pend_comb)
                if pend_comb % 2 == 1:
                    emit_cproj(pend_comb // 2)
                    pend_y = pend_comb // 2
                pend_comb = None
            if islat:
                emit_post_mults(bi)
                pend_comb = bi
        if pend_y is not None:
            emit_y(pend_y)
            pend_y = None
        if pend_comb is not None:
            emit_combine(pend_comb)
            emit_cproj(pend_comb // 2)
            emit_y(pend_comb // 2)
    P.pop()
    if int(os.environ.get('PH_STOP', '99')) < 3:
        P.barrier(); P.emit(); return nc
    gwb = P.sb("gwb", [128, 4, 1024], BF16)
    gb = P.sb("gb", [128, 8])
    sig = P.sb("sig", [128, 512]); ym = [P.sb("ym%d" % i, [128, 512], BF16) for i in range(2)]
    ysq = P.sb("ysq", [128, 512], BF16)
    psg = [P.ps("psg%d" % i, [128, 512]) for i in range(2)]
    pss = P.ps("pss", [128, 4, 32])
    P.dma("pool", lambda e: e.dma_start(out=gwb[:], in_=glu_w.rearrange("(kc p) n -> p kc n", p=128)), "D_gwb", writes=["gwb"])
    P.dma("sp", lambda e: e.dma_start(out=gb[:], in_=glu_b), "D_gb", writes=["gb"])
    for tb in range(8):
        c0 = tb * 512
        for et in range(4):
            for j, col in enumerate((et, et + 4)):
                for kc in range(4):
                    P.op("pe", lambda e, j=j, col=col, kc=kc, c0=c0: e.matmul(psg[j][:], lhsT=gwb[:, kc, col * 128:(col + 1) * 128], rhs=ybuf[:, kc, c0:c0 + 512], start=(kc == 0), stop=(kc == 3)),
                         reads=["gwb", "ybuf"], writes=[psg[j].name])
            y_ = ym[et % 2]
            P.op("act", lambda e, et=et: e.activation(out=sig[:], in_=psg[1][:], func=AF.Sigmoid, bias=gb[:, et + 4:et + 5]), reads=[psg[1].name, "gb"], writes=["sig"])
            P.op("dve", lambda e, et=et, y_=y_: e.scalar_tensor_tensor(out=y_[:], in0=psg[0][:], scalar=gb[:, et:et + 1], in1=sig[:], op0=ALU.add, op1=ALU.mult), reads=[psg[0].name, "gb", "sig"], writes=[y_.name])
            P.dma("act", lambda e, et=et, c0=c0, y_=y_: e.dma_start(out=ymix_d[et * 128:(et + 1) * 128, c0:c0 + 512], in_=y_[:]), "DS_" + y_.name, reads=[y_.name], writes=["ymix_d"])
            P.op("pool", lambda e, y_=y_: e.tensor_tensor(out=ysq[:], in0=y_[:], in1=y_[:], op=ALU.mult), reads=[y_.name], writes=["ysq"])
            for sub in range(4):
                tt = tb * 4 + sub
                P.op("pe", lambda e, sub=sub, tt=tt, et=et: e.matmul(pss[:, et, tt:tt + 1], lhsT=ysq[:, sub * 128:(sub + 1) * 128], rhs=ones[:], start=True, stop=True),
                     reads=["ysq", "ones"], writes=["pss"])
    P.op("dve", lambda e: e.tensor_copy(out=rs[:], in_=pss[:, 0, :]), reads=["pss"], writes=["rs"])
    P.op("dve", lambda e: e.tensor_tensor(out=rs[:], in0=rs[:], in1=pss[:, 1, :], op=ALU.add), reads=["pss", "rs"], writes=["rs"])
    P.op("dve", lambda e: e.tensor_tensor(out=rs[:], in0=rs[:], in1=pss[:, 2, :], op=ALU.add), reads=["pss", "rs"], writes=["rs"])
    P.op("dve", lambda e: e.tensor_tensor(out=rs[:], in0=rs[:], in1=pss[:, 3, :], op=ALU.add), reads=["pss", "rs"], writes=["rs"])
    P.op("act", lambda e: e.activation(out=rs[:], in_=rs[:], func=AF.Sqrt, scale=1.0 / 512, bias=EPS), reads=["rs"], writes=["rs"])
    P.op("dve", lambda e: e.reciprocal(out=rs[:], in_=rs[:]), reads=["rs"], writes=["rs"])
    P.pop()
    P.pop()

    if int(os.environ.get('PH_STOP', '99')) < 4:
        P.barrier(); P.emit(); return nc
    NBT = 8
    fx_d = dscr("fx_d", [2, 2, NBT, 64, L], BF16)
    xl_d = dscr("xl_d", [3, NBT, 64, L], BF16)
    hq_d = dscr("hq_d", [2, NBT, 128, 128 * 64], BF16)
    yx_d = dscr("yx_d", [NBT, 64, L], BF16)
    P.push()
    fv = P.sb("fv", [64, 3]); fsc = P.sb("fsc", [64, 4])
    w3s = P.sb("w3s", [64, 2048])
    h2 = [P.sb("h2T%d" % i, [64, L], BF16) for i in range(2)]
    w3b = P.sb("w3b", [64, 2048], BF16)
    psf = [P.ps("psf%d" % i, [128, 512]) for i in range(2)]
    P.dma("sp", lambda e: e.dma_start(out=fv[:], in_=fvec), "D_fv", writes=["fv"])
    P.dma("sp", lambda e: e.dma_start(out=w3s[:], in_=fw3), "D_w3s", writes=["w3s"])
    P.op("act", lambda e: e.copy(out=w3b[:], in_=w3s[:]), reads=["w3s"], writes=["w3b"])
    P.op("dve", lambda e: e.tensor_scalar(out=fsc[:, 0:1], in0=fv[:, 2:3], scalar1=1.0 / 3, scalar2=None, op0=ALU.mult), reads=["fv"], writes=["fsc"])
    P.op("dve", lambda e: e.tensor_tensor(out=fsc[:, 1:3], in0=fv[:, 0:2], in1=fsc[:, 0:1].to_broadcast([64, 2]), op=ALU.mult), reads=["fv", "fsc"], writes=["fsc"])
    P.push()
    emb = P.sb("emb", [33, L]); w1s = P.sb("w1s", [33, 64]); w2s = P.sb("w2s", [64, 64])
    h1T = P.sb("h1T", [64, L]); stmp = P.sb("stmp", [64, 512]); stm2 = P.sb("stm2", [64, 512])
    P.dma("sp", lambda e: e.dma_start(out=w1s[:], in_=fw1), "D_w1s", writes=["w1s"])
    P.dma("sp", lambda e: e.dma_start(out=w2s[:], in_=fw2), "D_w2s", writes=["w2s"])
    for var in range(2):
        P.dma("sp", lambda e, var=var: e.dma_start(out=emb[:], in_=embT[var]), "D_emb", writes=["emb"])
        for layer, (wsrc, src, dst, bcol) in enumerate(((w1s, emb, h1T, 1), (w2s, h1T, h2[var], 2))):
            for cbk in range(8):
                pf = psf[cbk % 2]
                P.op("pe", lambda e, pf=pf, wsrc=wsrc, src=src, cbk=cbk: e.matmul(pf[0:64, :], lhsT=wsrc[:], rhs=src[:, cbk * 512:(cbk + 1) * 512], start=True, stop=True),
                     reads=[wsrc.name, src.name], writes=[pf.name])
                P.op("act", lambda e, pf=pf, bcol=bcol: e.activation(out=stmp[:], in_=pf[0:64, :], func=AF.Sin, scale=fsc[:, 0:1], bias=fsc[:, bcol:bcol + 1]), reads=[pf.name, "fsc"], writes=["stmp"])
                P.op("dve", lambda e: e.tensor_tensor(out=stm2[:], in0=stmp[:], in1=stmp[:], op=ALU.mult), reads=["stmp"], writes=["stm2"])
                P.op("dve", lambda e: e.tensor_scalar(out=stm2[:], in0=stm2[:], scalar1=-4.0, scalar2=3.0, op0=ALU.mult, op1=ALU.add), reads=["stm2"], writes=["stm2"])
                P.op("dve", lambda e, dst=dst, cbk=cbk: e.tensor_tensor(out=dst[:, cbk * 512:(cbk + 1) * 512], in0=stm2[:], in1=stmp[:], op=ALU.mult), reads=["stm2", "stmp"], writes=[dst.name])
    P.pop()
    tau = P.sb("tau", [64, 2, 64])
    dneg = P.sb("dneg", [64, 2, 512]); brow = P.sb("brow", [1, 2, 512])
    warg = P.sb("warg", [64, 64, 64]); wwin = P.sb("wwin", [64, 64, 64])
    fxt = [P.sb("fxt%d" % i, [64, 64, 64], BF16) for i in range(2)]
    P.dma("sp", lambda e: e.dma_start(out=tau[:], in_=tauX), "D_tau", writes=["tau"])
    P.dma("sp", lambda e: e.dma_start(out=dneg[:].rearrange("p o c -> p (o c)"), in_=hdec_row.to_broadcast([64, 1024])), "D_dneg", writes=["dneg"])
    P.dma("sp", lambda e: e.dma_start(out=brow[:].rearrange("p o c -> p (o c)"), in_=hbias_row), "D_brow", writes=["brow"])
    P.op("act", lambda e: e.activation(out=dneg[:], in_=dneg[:], func=AF.Abs), reads=["dneg"], writes=["dneg"])
    P.op("dve", lambda e: e.tensor_scalar(out=dneg[:], in0=dneg[:], scalar1=-1.0, scalar2=None, op0=ALU.mult), reads=["dneg"], writes=["dneg"])
    kf = 0
    for o in range(2):
        for bt in range(NBT):
            cg = bt * 64
            for d_ in range(2):
                P.op("dve", lambda e, o=o, cg=cg, d_=d_: e.tensor_tensor(out=warg[:], in0=tau[:, d_, :].unsqueeze(2).to_broadcast([64, 64, 64]),
                                                                 in1=dneg[:, o, cg:cg + 64].unsqueeze(1).to_broadcast([64, 64, 64]), op=ALU.mult),
                     reads=["tau", "dneg"], writes=["warg"])
                P.op("act", lambda e: e.activation(out=wwin[:], in_=warg[:], func=AF.Exp), reads=["warg"], writes=["wwin"])
                ft_ = fxt[kf % 2]; kf += 1
                col = o * 1024 + d_ * 512 + cg
                for n8 in range(8):
                    pf = psf[n8 % 2]
                    for j in range(8):
                        n2 = n8 * 8 + j
                        P.op("pe", lambda e, pf=pf, j=j, n2=n2, d_=d_, col=col: e.matmul(pf[0:64, j * 64:(j + 1) * 64], lhsT=h2[d_][:, n2:L:64], rhs=w3b[:, col:col + 64], start=True, stop=True),
                             reads=[h2[d_].name, "w3b"], writes=[pf.name])
                    pv = pf[0:64, :].rearrange("p (j c) -> p j c", j=8)
                    if d_ == 0:
                        P.op("dve", lambda e, pv=pv, ft_=ft_, n8=n8: e.tensor_tensor(out=ft_[:, :, n8 * 8:(n8 + 1) * 8].rearrange("p c n -> p n c"), in0=pv, in1=wwin[:, n8 * 8:(n8 + 1) * 8, :], op=ALU.mult),
                             reads=[pf.name, "wwin"], writes=[ft_.name])
                    else:
                        P.op("dve", lambda e, pv=pv, ft_=ft_, n8=n8: e.scalar_tensor_tensor(out=ft_[:, :, n8 * 8:(n8 + 1) * 8].rearrange("p c n -> p n c"), in0=pv, scalar=-1.0, in1=wwin[:, n8 * 8:(n8 + 1) * 8, :], op0=ALU.mult, op1=ALU.mult),
                             reads=[pf.name, "wwin"], writes=[ft_.name])
                if d_ == 0:
                    P.op("dve", lambda e, ft_=ft_, o=o, cg=cg: e.tensor_tensor(out=ft_[0:1, :, 0], in0=ft_[0:1, :, 0], in1=brow[0:1, o, cg:cg + 64], op=ALU.add), reads=[ft_.name, "brow"], writes=[ft_.name])
                else:
                    P.op("dve", lambda e, ft_=ft_: e.memset(ft_[0:1, :, 0], 0.0), reads=[ft_.name], writes=[ft_.name])
                P.dma("act", lambda e, ft_=ft_, o=o, d_=d_, bt=bt: e.dma_start(out=fx_d[o, d_, bt], in_=ft_[:].rearrange("p a c -> p (a c)")), "DS_" + ft_.name, reads=[ft_.name], writes=["fx_d"])
    P.pop()
    if STOP < 1:
        P.barrier(); P.emit(); return nc
    P.push()
    cws = P.sb("cws", [128, 12, 3]); cbs = P.sb("cbs", [128, 12])
    zrs = [P.sb("zr%d" % i, [128, L]) for i in range(2)]; scfs = [P.sb("scf%d" % i, [128, L]) for i in range(2)]; scbs = [P.sb("scb%d" % i, [128, L], BF16) for i in range(2)]
    xtl = [P.sb("xtl%d" % i, [64, 2, 64, 64], BF16) for i in range(2)]
    pstx = [P.ps("pstx%d" % i, [64, 8, 128], BF16) for i in range(2)]
    P.dma("sp", lambda e: e.dma_start(out=cws[:], in_=cw), "D_cws", writes=["cws"])
    P.dma("sp", lambda e: e.dma_start(out=cbs[:], in_=cb), "D_cbs", writes=["cbs"])
    for ft in range(12):
        j_, ct = ft // 4, ft % 4
        zr = zrs[ft % 2]; scf = scfs[ft % 2]; scb = scbs[ft % 2]
        P.dma("sp", lambda e, ft=ft, zr=zr: e.dma_start(out=zr[:], in_=z_d[ft * 128:(ft + 1) * 128, :]), "D_" + zr.name, reads=["z_d"], writes=[zr.name])
        P.op("dve", lambda e, ft=ft, zr=zr, scf=scf: e.tensor_scalar(out=scf[:], in0=zr[:], scalar1=cws[:, ft, 1:2], scalar2=cbs[:, ft:ft + 1], op0=ALU.mult, op1=ALU.add), reads=[zr.name, "cws", "cbs"], writes=[scf.name])
        P.op("dve", lambda e, ft=ft, zr=zr, scf=scf: e.scalar_tensor_tensor(out=scf[:, 1:L], in0=zr[:, 0:L - 1], scalar=cws[:, ft, 0:1], in1=scf[:, 1:L], op0=ALU.mult, op1=ALU.add), reads=[zr.name, "cws", scf.name], writes=[scf.name])
        P.op("dve", lambda e, ft=ft, zr=zr, scf=scf: e.scalar_tensor_tensor(out=scf[:, 0:L - 1], in0=zr[:, 1:L], scalar=cws[:, ft, 2:3], in1=scf[:, 0:L - 1], op0=ALU.mult, op1=ALU.add), reads=[zr.name, "cws", scf.name], writes=[scf.name])
        P.op("act", lambda e, scf=scf, scb=scb: e.copy(out=scb[:], in_=scf[:]), reads=[scf.name], writes=[scb.name])
        xt_ = xtl[ft % 2]
        for n8 in range(8):
            px = pstx[n8 % 2]
            for j in range(8):
                n2 = n8 * 8 + j
                P.op("pe", lambda e, px=px, j=j, n2=n2, scb=scb: e.transpose(out=px[:, j, :], in_=scb[:, n2:L:64], identity=ident[:]), reads=[scb.name, "ident"], writes=[px.name])
            for h in range(2):
                eng = "act" if h == 0 else "dve"
                if eng == "act":
                    P.op("act", lambda e, px=px, xt_=xt_, h=h, n8=n8: e.copy(out=xt_[:, h, :, n8 * 8:(n8 + 1) * 8].rearrange("p c n -> p n c"), in_=px[:, :, h * 64:(h + 1) * 64]), reads=[px.name], writes=[xt_.name])
                else:
                    P.op("dve", lambda e, px=px, xt_=xt_, h=h, n8=n8: e.tensor_copy(out=xt_[:, h, :, n8 * 8:(n8 + 1) * 8].rearrange("p c n -> p n c"), in_=px[:, :, h * 64:(h + 1) * 64]), reads=[px.name], writes=[xt_.name])
        for h in range(2):
            P.dma("act", lambda e, xt_=xt_, j_=j_, ct=ct, h=h: e.dma_start(out=xl_d[j_, ct * 2 + h], in_=xt_[:, h].rearrange("p a c -> p (a c)")), "DS_%s%d" % (xt_.name, h), reads=[xt_.name], writes=["xl_d"])
    P.pop()
    if STOP < 2:
        P.barrier(); P.emit(); return nc
    P.push()
    Gs = P.sb("Gs", [128, 128, 2, 128], BF16)
    GPs = P.sb("GPs", [128, 128, 128], BF16)
    F1s = P.sb("F1s", [64, 2, 256], BF16)
    Es = P.sb("Es", [128, 2, 64], BF16)
    xin = P.sb("xin", [64, 64, 64], BF16); gin = P.sb("gin", [64, 64, 64], BF16); gat = P.sb("gat", [64, 64, 64], BF16)
    AB = P.sb("AB", [128, 16384], BF16)
    Abuf = AB[0:64, :].rearrange("p (r k c) -> p r k c", r=2, k=128)
    Bb = AB[:, 0:8192].rearrange("p (s k) -> p s k", k=128)
    PQ = P.sb("PQ", [128, 8192], BF16)
    Pq = PQ[:].rearrange("p (k s) -> p k s", s=64)
    BT = PQ[:].rearrange("p (r c n) -> p r c n", r=2, c=64)
    hqs = [P.sb("hqs%d" % i, [128, 8, 64], BF16) for i in range(6)]
    ps1 = [P.ps("ps1%d" % i, [64, 2, 256]) for i in range(2)]
    ps2 = [P.ps("ps2%d" % i, [128, 8, 64]) for i in range(2)]
    psB = P.ps("psB", [128, 8, 64])
    psT = P.ps("psT", [128, 8, 128], BF16)
    psA = [P.ps("psA%d" % i, [64, 512]) for i in range(2)]
    P.dma("sp", lambda e: e.dma_start(out=Gs[:].rearrange("p a r m -> p (a r m)"), in_=Gtab), "D_Gs", writes=["Gs"])
    P.dma("sp", lambda e: e.dma_start(out=GPs[:].rearrange("p a m -> p (a m)"), in_=GPtab), "D_GPs", writes=["GPs"])
    P.dma("sp", lambda e: e.dma_start(out=F1s[:].rearrange("p a m -> p (a m)"), in_=F1tab), "D_F1s", writes=["F1s"])
    P.dma("sp", lambda e: e.dma_start(out=Es[:].rearrange("p a m -> p (a m)"), in_=Etab), "D_Es", writes=["Es"])
    ek = [0]

    def evac(fn_act, fn_dve, reads, writes):
        ek[0] += 1
        if ek[0] % 2 == 0:
            P.op("act", fn_act, reads=reads, writes=writes)
        else:
            P.op("dve", fn_dve, reads=reads, writes=writes)

    def stage_F1(srcs):
        for c2_ in range(32):
            p1 = ps1[c2_ % 2]
            for a in range(2):
                c = c2_ * 2 + a
                for si, (tl, var) in enumerate(srcs):
                    P.op("pe", lambda e, p1=p1, a=a, tl=tl, c=c, var=var, si=si: e.matmul(p1[:, a, :], lhsT=tl[:, c, :], rhs=F1s[:, var, :], start=(si == 0), stop=(si == len(srcs) - 1)),
                         reads=[tl.name, "F1s"], writes=[p1.name])
            ov = Abuf[:, :, :, c2_ * 2:c2_ * 2 + 2]
            iv = p1[:].rearrange("p a (r k) -> p r k a", r=2)
            evac(lambda e, ov=ov, iv=iv: e.copy(out=ov, in_=iv), lambda e, ov=ov, iv=iv: e.tensor_copy(out=ov, in_=iv), [p1.name], ["AB"])

    def stage_F2(consume):
        for k8 in range(16):
            p2 = ps2[k8 % 2]
            for kk in range(8):
                k1 = k8 * 8 + kk
                for ri in range(2):
                    P.op("pe", lambda e, p2=p2, kk=kk, k1=k1, ri=ri: e.matmul(p2[:, kk, :], lhsT=Gs[0:64, k1, ri, :], rhs=Abuf[:, ri, k1, :], start=(ri == 0), stop=(ri == 1)),
                         reads=["Gs", "AB"], writes=[p2.name])
            consume(k8, p2)

    MAINSTOP = int(os.environ.get('HY_MAIN', '9999'))
    mstep = [0]

    def chk():
        mstep[0] += 1
        return mstep[0] >= MAINSTOP
    for bt in range(NBT):
        for o in range(2):
            P.dma("sp", lambda e, o=o, bt=bt: e.dma_start(out=xin[:].rearrange("p a c -> p (a c)"), in_=fx_d[o, 0, bt]), "D_xin", reads=["fx_d"], writes=["xin"])
            P.dma("sp", lambda e, o=o, bt=bt: e.dma_start(out=gin[:].rearrange("p a c -> p (a c)"), in_=fx_d[o, 1, bt]), "D_gin", reads=["fx_d"], writes=["gin"])
            stage_F1([(xin, 0), (gin, 1)])

            def cons_h(k8, p2, o=o, bt=bt):
                hq = hqs[k8 % 6]
                iv = p2[:]
                evac(lambda e, hq=hq, iv=iv: e.copy(out=hq[:], in_=iv), lambda e, hq=hq, iv=iv: e.tensor_copy(out=hq[:], in_=iv), [p2.name], [hq.name])
                P.dma("sp", lambda e, hq=hq, k8=k8: e.dma_start(out=hq_d[o, bt, :, k8 * 512:(k8 + 1) * 512], in_=hq[:].rearrange("p k s -> p (k s)")), "DS_" + hq.name, reads=[hq.name], writes=["hq_d%d_%d_%d" % (o, bt, k8)])
            stage_F2(cons_h)
            if chk():
                P.barrier(); P.emit(); return nc
        for o in range(2):
            if o == 0:
                P.dma("sp", lambda e, bt=bt: e.dma_start(out=xin[:].rearrange("p a c -> p (a c)"), in_=xl_d[0, bt]), "D_xin", reads=["xl_d"], writes=["xin"])
            P.dma("sp", lambda e, o=o, bt=bt: e.dma_start(out=gat[:].rearrange("p a c -> p (a c)"), in_=xl_d[1 + o, bt]), "D_gat", reads=["xl_d"], writes=["gat"])
            stage_F1([(xin, 0)])

            def cons_x(k8, p2, o=o, bt=bt):
                hq = hqs[k8 % 6]
                srcv = hq_d[o, bt, :, k8 * 512:(k8 + 1) * 512]
                for (d0, s0, nrow, tg) in ((0, 0, 64, "a"), (64, 96, 32, "b"), (96, 64, 32, "c")):
                    P.dma("sp", lambda e, hq=hq, d0=d0, s0=s0, nrow=nrow, srcv=srcv: e.dma_start(out=hq[d0:d0 + nrow].rearrange("p k s -> p (k s)"), in_=srcv[s0:s0 + nrow, :]),
                          "D_%s%s" % (hq.name, tg), reads=["hq_d%d_%d_%d" % (o, bt, k8)], writes=[hq.name + tg])
                iv = p2[:]
                P.op("dve", lambda e, hq=hq, iv=iv, k8=k8: e.tensor_tensor(out=Pq[:, k8 * 8:(k8 + 1) * 8, :], in0=iv, in1=hq[:], op=ALU.mult),
                     reads=[p2.name, hq.name + "a", hq.name + "b", hq.name + "c"], writes=["PQ", hq.name])
            stage_F2(cons_x)
            if chk():
                P.barrier(); P.emit(); return nc
            for k8 in range(16):
                for kk in range(8):
                    k1 = k8 * 8 + kk
                    P.op("pe", lambda e, kk=kk, k1=k1: e.matmul(psB[:, kk, :], lhsT=GPs[:, k1, :], rhs=Pq[:, k1, :], start=True, stop=True), reads=["GPs", "PQ"], writes=["psB"])
                ov = Bb[:, :, k8 * 8:(k8 + 1) * 8].rearrange("p s k -> p k s")
                evac(lambda e, ov=ov: e.copy(out=ov, in_=psB[:]), lambda e, ov=ov: e.tensor_copy(out=ov, in_=psB[:]), ["psB"], ["AB"])
            if chk():
                P.barrier(); P.emit(); return nc
            for s8 in range(8):
                for j in range(8):
                    sl_ = s8 * 8 + j
                    P.op("pe", lambda e, j=j, sl_=sl_: e.transpose(out=psT[:, j, :], in_=Bb[:, sl_, :], identity=ident[:]), reads=["AB", "ident"], writes=["psT"])
                ov = BT[:, :, s8 * 8:(s8 + 1) * 8, :]
                iv = psT[:].rearrange("p s (r n) -> p r s n", r=2)
                evac(lambda e, ov=ov, iv=iv: e.copy(out=ov, in_=iv), lambda e, ov=ov, iv=iv: e.tensor_copy(out=ov, in_=iv), ["psT"], ["PQ"])
            if chk():
                P.barrier(); P.emit(); return nc
            dst = xin if o == 0 else gin
            for nb in range(8):
                pa = psA[nb % 2]
                for ri in range(2):
                    P.op("pe", lambda e, pa=pa, ri=ri, nb=nb: e.matmul(pa[:], lhsT=Es[:, ri, :], rhs=BT[:, ri, nb * 8:(nb + 1) * 8, :].rearrange("p c n -> p (c n)"), start=(ri == 0), stop=(ri == 1)),
                         reads=["Es", "PQ"], writes=[pa.name])
                P.op("dve", lambda e, pa=pa, dst=dst, nb=nb: e.tensor_tensor(out=dst[:, nb * 8:(nb + 1) * 8, :], in0=pa[:].rearrange("p (c n) -> p c n", c=8), in1=gat[:, nb * 8:(nb + 1) * 8, :], op=ALU.mult),
                     reads=[pa.name, "gat"], writes=[dst.name])
            if chk():
                P.barrier(); P.emit(); return nc
            if o == 1:
                P.dma("sp", lambda e, bt=bt: e.dma_start(out=yx_d[bt], in_=gin[:].rearrange("p a c -> p (a c)")), "DS_gin", reads=["gin"], writes=["yx_d%d" % bt])
    P.pop()
    if STOP < 3:
        P.barrier(); P.emit(); return nc
    P.push()
    Yt = P.sb("Yt", [64, 2, 64, 64], BF16)
    yhb = P.sb("yhb", [128, L], BF16); yhs = P.sb("yhs", [128, 512], BF16)
    psY = [P.ps("psY%d" % i, [128, 16, 64], BF16) for i in range(2)]
    psh = P.ps("psh", [128, 32])
    for ct in range(4):
        for h in range(2):
            P.dma("sp", lambda e, ct=ct, h=h: e.dma_start(out=Yt[:, h].rearrange("p a c -> p (a c)"), in_=yx_d[ct * 2 + h]), "D_Yt%d" % h, reads=["yx_d%d" % (ct * 2 + h)], writes=["Yt%d" % h])
        yv = yhb[:].rearrange("p (a b) -> p a b", b=64)
        for n16 in range(4):
            py = psY[n16 % 2]
            for j in range(16):
                n2 = n16 * 16 + j
                P.op("pe", lambda e, py=py, j=j, n2=n2: e.transpose(out=py[:, j, :], in_=Yt[:, :, :, n2].rearrange("p h c -> p (h c)"), identity=ident[0:64, 0:64]), reads=["Yt0", "Yt1", "ident"], writes=[py.name])
            ov = yv[:, :, n16 * 16:(n16 + 1) * 16]
            iv = py[:].rearrange("p j a -> p a j")
            evac(lambda e, ov=ov, iv=iv: e.copy(out=ov, in_=iv), lambda e, ov=ov, iv=iv: e.tensor_copy(out=ov, in_=iv), [py.name], ["yhb"])
        P.dma("sp", lambda e, ct=ct: e.dma_start(out=ymix_d[512 + ct * 128:512 + (ct + 1) * 128, :], in_=yhb[:]), "DS_yhb", reads=["yhb"], writes=["ymix_d"])
        for tb in range(8):
            P.op("pool", lambda e, tb=tb: e.tensor_tensor(out=yhs[:], in0=yhb[:, tb * 512:(tb + 1) * 512], in1=yhb[:, tb * 512:(tb + 1) * 512], op=ALU.mult), reads=["yhb"], writes=["yhs"])
            for sub in range(4):
                tt = tb * 4 + sub
                P.op("pe", lambda e, sub=sub, tt=tt: e.matmul(psh[:, tt:tt + 1], lhsT=yhs[:, sub * 128:(sub + 1) * 128], rhs=ones[:], start=True, stop=True), reads=["yhs", "ones"], writes=["psh"])
        P.op("dve", lambda e: e.tensor_tensor(out=ssh[:], in0=ssh[:], in1=psh[:], op=ALU.add), reads=["ssh", "psh"], writes=["ssh"])
    P.op("act", lambda e: e.activation(out=rh[:], in_=ssh[:], func=AF.Sqrt, scale=1.0 / 512, bias=EPS), reads=["ssh"], writes=["rh"])
    P.op("dve", lambda e: e.reciprocal(out=rh[:], in_=rh[:]), reads=["rh"], writes=["rh"])
    P.pop()

    if int(os.environ.get('PH_STOP', '99')) < 8:
        P.barrier(); P.emit(); return nc
    P.push()
    wob = P.sb("wob", [128, 8, D], BF16)
    mg = P.sb("mg", [128, 8])
    ymb = P.sb("ymb", [128, 8, 512], BF16)
    wst1 = [P.sb("wst1%d" % i, [128, 8, 1024], BF16) for i in range(2)]
    wst2 = wst1
    hid = P.sb("hid", [128, 32, 512], BF16)
    h1b = P.sb("h1b", [128, 4, D])
    hn2T = P.sb("hn2T", [128, 8, 512], BF16)
    xt3 = P.sb("xt3", [128, D]); pt3 = P.sb("pt3", [128, D])
    tA = P.sb("tA", [128, D]); hn = P.sb("hn3", [128, D]); hnb = P.sb("hnb3", [128, D], BF16)
    rl = P.sb("rl", [128, 512])
    ssq = P.sb("ssq3", [128, 1]); rstd = P.sb("rstd3", [128, 1])
    ot_ = P.sb("ot_", [128, D]); junk = ot_
    accS = P.ps("accS", [128, D]); accH = P.ps("accH", [128, D])
    pst = P.ps("pst3", [128, D], BF16)
    psa = [P.ps("psa%d" % i, [128, 512]) for i in range(2)]
    P.dma("pool", lambda e: e.dma_start(out=wob[:], in_=w_out.rearrange("(kc p) n -> p kc n", p=128)), "D_wob", writes=["wobraw"])
    P.dma("sp", lambda e: e.dma_start(out=mg[:], in_=mixg), "D_mg", writes=["mg"])
    for kc in range(8):
        P.op("pool", lambda e, kc=kc: e.tensor_scalar(out=wob[:, kc, :], in0=wob[:, kc, :], scalar1=mg[:, kc:kc + 1], scalar2=None, op0=ALU.mult), reads=["wobraw", "mg"], writes=["wob"])
    ymix_v = ymix_d.rearrange("(kc p) t -> p kc t", p=128)
    w1b_v = w1b_d.rearrange("(kc p) n -> p kc n", p=128)
    w2b_v = w2b_d.rearrange("(ft p) n -> p ft n", p=128)
    wk = 0
    for tb in range(8):
        c0 = tb * 512
        P.dma("sp", lambda e, c0=c0: e.dma_start(out=ymb[:], in_=ymix_v[:, :, c0:c0 + 512]), "D_ymb", reads=["ymix_d"], writes=["ymb"])
        for sub in range(4):
            tt = tb * 4 + sub
            for (acc_, k0) in ((accS, 0), (accH, 4)):
                for half in range(2):
                    for kc in range(4):
                        P.op("pe", lambda e, acc_=acc_, k0=k0, half=half, kc=kc, sub=sub: e.matmul(acc_[:, half * 512:(half + 1) * 512], lhsT=ymb[:, k0 + kc, sub * 128:(sub + 1) * 128], rhs=wob[:, k0 + kc, half * 512:(half + 1) * 512], start=(kc == 0), stop=(kc == 3)),
                             reads=["ymb", "wob"], writes=[acc_.name])
            P.dma("sp", lambda e, tt=tt: e.dma_start(out=xt3[:], in_=x[tt * 128:(tt + 1) * 128, :]), "D_xt3", writes=["xt3"])
            P.dma("sp", lambda e, tt=tt: e.dma_start(out=pt3[:], in_=pos[tt * 128:(tt + 1) * 128, :]), "D_pt3", writes=["pt3"])
            P.op("pool", lambda e: e.tensor_tensor(out=xt3[:], in0=xt3[:], in1=pt3[:], op=ALU.add), reads=["xt3", "pt3"], writes=["xt3"])
            P.op("dve", lambda e, tt=tt: e.tensor_scalar(out=tA[:], in0=accS[:], scalar1=rs[:, tt:tt + 1], scalar2=None, op0=ALU.mult), reads=["accS", "rs"], writes=["tA"])
            P.op("dve", lambda e, tt=tt: e.scalar_tensor_tensor(out=tA[:], in0=accH[:], scalar=rh[:, tt:tt + 1], in1=tA[:], op0=ALU.mult, op1=ALU.add), reads=["accH", "rh", "tA"], writes=["tA"])
            P.op("pool", lambda e: e.tensor_tensor(out=tA[:], in0=tA[:], in1=G1[:], op=ALU.mult), reads=["tA", "G1"], writes=["tA"])
            P.op("pool", lambda e, sub=sub: e.tensor_tensor(out=h1b[:, sub, :], in0=tA[:], in1=xt3[:], op=ALU.add), reads=["tA", "xt3"], writes=["h1b%d" % sub])
            h1s = h1b[:, sub, :]
            P.op("act", lambda e, h1s=h1s: e.activation(out=junk[:], in_=h1s, func=AF.Square, accum_out=ssq[:]), reads=["h1b%d" % sub], writes=["ot_", "ssq3"])
            P.op("act", lambda e: e.activation(out=rstd[:], in_=ssq[:], func=AF.Sqrt, scale=1.0 / D, bias=EPS), reads=["ssq3"], writes=["rstd3"])
            P.op("dve", lambda e: e.reciprocal(out=rstd[:], in_=rstd[:]), reads=["rstd3"], writes=["rstd3"])
            P.op("dve", lambda e, h1s=h1s: e.scalar_tensor_tensor(out=hn[:], in0=h1s, scalar=rstd[:, 0:1], in1=A2[:], op0=ALU.mult, op1=ALU.mult), reads=["h1b%d" % sub, "rstd3", "A2"], writes=["hn3"])
            P.op("pool", lambda e: e.tensor_tensor(out=hnb[:], in0=hn[:], in1=B2[:], op=ALU.add), reads=["hn3", "B2"], writes=["hnb3"])
            for kc in range(8):
                P.op("pe", lambda e, kc=kc: e.transpose(out=pst[:, kc * 128:(kc + 1) * 128], in_=hnb[:, kc * 128:(kc + 1) * 128], identity=ident[:]), reads=["hnb3", "ident"], writes=["pst3"])
            P.op("act", lambda e, sub=sub: e.copy(out=hn2T[:, :, sub * 128:(sub + 1) * 128], in_=pst[:].rearrange("p (k t) -> p k t", k=8)), reads=["pst3"], writes=["hn2T"])
        for fg_ in range(4):
            ws = wst1[wk % 2]; wk += 1
            P.dma("sp", lambda e, ws=ws, fg_=fg_: e.dma_start(out=ws[:], in_=w1b_v[:, :, fg_ * 1024:(fg_ + 1) * 1024]), "D_" + ws.name, reads=["w1b_d"], writes=[ws.name])
            for f8 in range(8):
                ft = fg_ * 8 + f8
                pa = psa[ft % 2]
                for kc in range(8):
                    P.op("pe", lambda e, pa=pa, ws=ws, f8=f8, kc=kc: e.matmul(pa[:], lhsT=ws[:, kc, f8 * 128:(f8 + 1) * 128], rhs=hn2T[:, kc, :], start=(kc == 0), stop=(kc == 7)),
                         reads=[ws.name, "hn2T"], writes=[pa.name])
                P.op("dve", lambda e, pa=pa: e.tensor_scalar(out=rl[:], in0=pa[:], scalar1=0.0, scalar2=None, op0=ALU.max), reads=[pa.name], writes=["rl"])
                P.op("act", lambda e, ft=ft: e.activation(out=hid[:, ft, :], in_=rl[:], func=AF.Square), reads=["rl"], writes=["hid"])
        kq = 0
        for fg_ in range(4):
            ws = wst2[wk % 2]; wk += 1
            P.dma("sp", lambda e, ws=ws, fg_=fg_: e.dma_start(out=ws[:], in_=w2b_v[:, fg_ * 8:(fg_ + 1) * 8, :]), "D_" + ws.name, reads=["w2b_d"], writes=[ws.name])
            for sub in range(4):
                a_ = accS if kq % 2 == 0 else accH
                tmp_ = tA if kq % 2 == 0 else hn
                kq += 1
                for half in range(2):
                    for f8 in range(8):
                        ft = fg_ * 8 + f8
                        P.op("pe", lambda e, a_=a_, sub=sub, half=half, f8=f8, ft=ft, ws=ws: e.matmul(a_[:, half * 512:(half + 1) * 512], lhsT=hid[:, ft, sub * 128:(sub + 1) * 128], rhs=ws[:, f8, half * 512:(half + 1) * 512], start=(f8 == 0), stop=(f8 == 7)),
                             reads=["hid", ws.name], writes=[a_.name])
                P.op("dve", lambda e, a_=a_, tmp_=tmp_: e.tensor_tensor(out=tmp_[:], in0=a_[:], in1=G2[:], op=ALU.mult), reads=[a_.name, "G2"], writes=[tmp_.name])
                P.op("pool", lambda e, sub=sub, tmp_=tmp_: e.tensor_tensor(out=h1b[:, sub, :], in0=h1b[:, sub, :], in1=tmp_[:], op=ALU.add), reads=[tmp_.name, "h1b%d" % sub], writes=["h1b%d" % sub])
        for sub in range(4):
            tt = tb * 4 + sub
            h2s = h1b[:, sub, :]
            P.op("act", lambda e, h2s=h2s: e.activation(out=junk[:], in_=h2s, func=AF.Square, accum_out=ssq[:]), reads=["h1b%d" % sub], writes=["ot_", "ssq3"])
            P.op("act", lambda e: e.activation(out=rstd[:], in_=ssq[:], func=AF.Sqrt, scale=1.0 / D, bias=EPS), reads=["ssq3"], writes=["rstd3"])
            P.op("dve", lambda e: e.reciprocal(out=rstd[:], in_=rstd[:]), reads=["rstd3"], writes=["rstd3"])
            P.op("dve", lambda e, h2s=h2s: e.scalar_tensor_tensor(out=ot_[:], in0=h2s, scalar=rstd[:, 0:1], in1=FG[:], op0=ALU.mult, op1=ALU.mult), reads=["h1b%d" % sub, "rstd3", "FG"], writes=["ot_"])
            P.dma("act", lambda e, tt=tt: e.dma_start(out=out[tt * 128:(tt + 1) * 128, :], in_=ot_[:]), "DS_ot", reads=["ot_"], writes=["out"])
    P.pop()
    P.barrier()
    P.emit()
    return nc


def _consts():
    n = L
    rows = n // 64
    row = np.repeat(np.arange(rows, dtype=np.float32), 64)
    col = np.tile(np.arange(64, dtype=np.float32), rows)
    quarter = D // 4
    omega = (1.0 / (np.float32(10000.0) ** (np.arange(quarter, dtype=np.float32) / np.float32(quarter)))).astype(np.float32)

    def enc(p):
        ang = (p[:, None] * omega[None, :]).astype(np.float32)
        return np.concatenate([np.sin(ang), np.cos(ang)], axis=-1)
    pos = np.concatenate([enc(row), enc(col)], axis=-1).astype(np.float32)
    t = np.linspace(0.0, 1.0, n, dtype=np.float32)[:, None]
    w = (np.float32(2.0 * math.pi) * np.arange(n, dtype=np.float32) / np.float32(n)).astype(np.float32)
    bands = np.linspace(1e-4, 15, 16, dtype=np.float32)
    ang = (w[:, None] * bands[None, :]).astype(np.float32)
    emb = np.concatenate([t, np.cos(ang), -np.sin(ang)], axis=-1).astype(np.float32)
    embT = np.ascontiguousarray(emb.T)
    embTr = embT.copy()
    embTr[:, 1:] = embT[:, :0:-1]
    embT2 = np.ascontiguousarray(np.stack([embT, embTr], 0))
    tt_ = t[:, 0]
    tau = np.zeros((64, 2, 64), np.float32)
    tau[:, 0, :] = tt_.reshape(64, 64)
    tr = np.empty(n, np.float32); tr[0] = 1.0; tr[1:] = tt_[:0:-1]
    tau[:, 1, :] = tr.reshape(64, 64)
    return pos, embT2, tau


def _hy_tables():
    import ml_dtypes
    N = 8192
    n1 = np.arange(64)[:, None]; k1 = np.arange(128)[None, :]
    ph = 2 * np.pi * (k1 + 0.5) * n1 / 128.0
    F1lo = np.concatenate([np.cos(ph), -np.sin(ph)], 1)
    ph2 = 2 * np.pi * (k1 + 0.5) * (n1 + 64) / 128.0
    F1hi = np.concatenate([np.cos(ph2), -np.sin(ph2)], 1)
    F1 = np.stack([F1lo, F1hi], 1).reshape(64, 512)
    n2 = np.arange(64)[:, None, None]; kk1 = np.arange(128)[None, :, None]; k2 = np.arange(32)[None, None, :]
    th = 2 * np.pi * n2 * (kk1 + 0.5 + 128 * k2) / 8192.0
    Gr = np.cos(th); Gi = -np.sin(th)
    SA = np.concatenate([Gr, Gi, Gi, Gr], 2)
    SB = np.concatenate([-Gi, Gr, Gr, -Gi], 2)
    G = np.stack([SA, SB], 2)
    G = np.concatenate([G, G], 0).reshape(128, 128 * 2 * 128)
    thT = np.transpose(th, (2, 1, 0))
    gr = np.cos(thT); gi = np.sin(thT)
    col_re = np.concatenate([gr, -gr, -gi, -gi], 0)
    col_im = np.concatenate([gi, -gi, gr, gr], 0)
    GP = np.concatenate([col_re, col_im], 2).reshape(128, 128 * 128)
    k1c = np.arange(128)[:, None]; n1r = np.arange(64)[None, :]
    ph3 = 2 * np.pi * (k1c + 0.5) * n1r / 128.0
    E = (np.stack([np.cos(ph3), -np.sin(ph3)], 1) * (2.0 / N)).reshape(128, 128)
    bf = lambda a: np.ascontiguousarray(a.astype(np.float32).astype(ml_dtypes.bfloat16))
    return bf(F1), bf(G), bf(GP), bf(E)


def kernel(**inp):
    f = lambda a: np.ascontiguousarray(np.asarray(a, dtype=np.float32))
    pos, embT2, tauX = _consts()
    F1t, Gt, GPt, Et = _hy_tables()
    B = 8
    x = f(inp["x"]); c = f(inp["c"]); ctx = f(inp["ctx"]); c_ctx = f(inp["c_ctx"])
    a_re = f(inp["s5_a_re"])[0]; a_im = f(inp["s5_a_im"])[0]; ls = f(inp["s5_log_step"])[0]
    b_re = f(inp["s5_b_re"])[0]; b_im = f(inp["s5_b_im"])[0]; c_re = f(inp["s5_c_re"])[0]; c_im = f(inp["s5_c_im"])[0]
    s5p = np.zeros((2, 3, 128, 16), np.float32)
    bpad = np.zeros((2, 128, 32, 2, 128), np.float32)
    clay = np.zeros((2, 2, 128, 16, 16), np.float32)
    for d_ in range(2):
        for g in range(32):
            gh, g16 = g // 16, g % 16
            s5p[d_, 0, gh * 64:(gh + 1) * 64, g16] = a_re[d_, g]
            s5p[d_, 1, gh * 64:(gh + 1) * 64, g16] = a_im[d_, g]
            s5p[d_, 2, gh * 64:(gh + 1) * 64, g16] = ls[d_, g]
            r0 = (g % 8) * 16
            bpad[d_, r0:r0 + 16, g, 0, gh * 64:(gh + 1) * 64] = b_re[d_, g].T
            bpad[d_, r0:r0 + 16, g, 1, gh * 64:(gh + 1) * 64] = b_im[d_, g].T
            clay[d_, 0, gh * 64:(gh + 1) * 64, g16, :] = c_re[d_, g].T
            clay[d_, 1, gh * 64:(gh + 1) * 64, g16, :] = c_im[d_, g].T
    bpad = bpad.reshape(2, 128, 32 * 2 * 128)
    clay = clay.reshape(2, 2, 128, 256)
    fm = lambda v, nt: np.ascontiguousarray(np.asarray(v, np.float32).reshape(nt, 128).T)
    cwv = f(inp["hy_conv_w"])[0]
    cw = np.ascontiguousarray(cwv.reshape(3, 12, 128).transpose(2, 1, 0))
    cb = fm(f(inp["hy_conv_b"])[0], 12)
    dec = f(inp["hy_decay"])[0]; hbv = f(inp["hy_bias"])[0]
    hdec_row = np.ascontiguousarray(dec.reshape(1, 1024))
    hbias_row = np.ascontiguousarray(hbv.reshape(1, 1024))
    fvec = np.ascontiguousarray(np.stack([f(inp["hy_f_b1"])[0], f(inp["hy_f_b2"])[0], f(inp["hy_f_freq"])[0]], axis=1))
    mixg = fm(np.concatenate([f(inp["mix_g_s5"])[0], f(inp["mix_g_hy"])[0]]), 8)
    shared = {
        "pos": pos, "embT": embT2, "tauX": tauX, "Gtab": Gt, "GPtab": GPt, "F1tab": F1t, "Etab": Et,
        "hdec_row": hdec_row, "hbias_row": hbias_row,
        "ada_w": f(inp["ada_w"])[0], "ada_b": f(inp["ada_b"]),
        "g1": f(inp["norm1_g"]), "g2": f(inp["norm2_g"]), "fg": f(inp["final_g"])[None, :],
        "w_in": f(inp["w_in"])[0], "s5p": s5p, "bpad": bpad, "clay": clay,
        "s5d": fm(f(inp["s5_d"])[0], 4), "glu_w": f(inp["s5_glu_w"])[0], "glu_b": fm(f(inp["s5_glu_b"])[0], 8),
        "cw": cw, "cb": cb, "fw1": f(inp["hy_f_w1"])[0], "fw2": f(inp["hy_f_w2"])[0], "fw3": f(inp["hy_f_w3"])[0],
        "fvec": fvec, "mixg": mixg,
        "w_out": f(inp["w_out"])[0], "w1": f(inp["mlp_w1"])[0], "w2": f(inp["mlp_w2"])[0],
    }
    in_maps = []
    for b in range(B):
        cc = np.stack([c[b], c_ctx], axis=1).reshape(8, 128, 2).transpose(1, 0, 2)
        m = dict(shared)
        m["x"] = x[b]; m["ctx"] = ctx[b]; m["cc"] = np.ascontiguousarray(cc)
        in_maps.append(m)
    nc = build_program()
    res = run_bass_kernel_spmd(nc, in_maps, core_ids=list(range(B)))
    return np.stack([np.asarray(r["out"], dtype=np.float32) for r in res.results], axis=0)
```
